# Optimizing a Trainium2 kernel written in Bass

```python
import numpy as np
import jax
import jax.numpy as jnp
from jax import lax

D_MODEL = 2048
BATCH = 1
SEQ = 16384
DEPTH = 2

HEAD_DIM = 64
N_NSA_HEADS = D_MODEL // (2 * HEAD_DIM)
N_NSA_KV = N_NSA_HEADS // 4
NSA_HPG = N_NSA_HEADS // N_NSA_KV
N_SB_HEADS = D_MODEL // (4 * HEAD_DIM)
N_FOX_HEADS = D_MODEL // (4 * HEAD_DIM)
MIX_WIDTH = (N_NSA_HEADS + N_SB_HEADS + N_FOX_HEADS) * HEAD_DIM
CMP_LEN = 32
CMP_STRIDE = 16
SLC_LEN = 64
SLC_TOPK = 8
WINDOW = 512
Q_BLOCK = 128
N_GROUPS = 4
EXPERTS_PER_GROUP = 4
N_EXPERTS = N_GROUPS * EXPERTS_PER_GROUP
INNER_TOPK = 2
D_FF_EXPERT = D_MODEL // 8
ALPHA = (2 * DEPTH) ** 0.25
BETA_INIT = (8 * DEPTH) ** -0.25
EPS = 1e-5
NEG_BIG = -1e30
SEL_BONUS = 1e6

NSA_Q_W = N_NSA_HEADS * HEAD_DIM
NSA_KV_W = N_NSA_KV * HEAD_DIM
SB_W = N_SB_HEADS * HEAD_DIM
FOX_W = N_FOX_HEADS * HEAD_DIM
PROJ_SIZES = (NSA_Q_W, NSA_KV_W, NSA_KV_W, NSA_KV_W, NSA_KV_W, NSA_KV_W, NSA_KV_W, 3 * N_NSA_HEADS, SB_W, SB_W, SB_W, FOX_W, FOX_W, FOX_W, N_FOX_HEADS)
V_PARTS = (2, 4, 6, 10, 13)
PROJ_DIM = sum(PROJ_SIZES)

kernel_name = 'hybrid_nsa_stickbreak_fox_hmoe'

F32 = jnp.float32


def layer_norm(x, g, b):
    xf = x.astype(F32)
    mu = jnp.mean(xf, axis=-1, keepdims=True)
    var = jnp.mean(jnp.square(xf - mu), axis=-1, keepdims=True)
    return ((xf - mu) * lax.rsqrt(var + EPS) * g + b).astype(x.dtype)


def rms_norm(x, g):
    xf = x.astype(F32)
    return (xf * lax.rsqrt(jnp.mean(jnp.square(xf), axis=-1, keepdims=True) + EPS) * g).astype(x.dtype)


def alibi_slopes(n):
    return 2.0 ** (-8.0 * jnp.arange(1, n + 1, dtype=F32) / n)


def masked_softmax(s, mask):
    return jax.nn.softmax(jnp.where(mask, s, NEG_BIG), axis=-1)


def strict_lower(n):
    r = jnp.arange(n)
    return (r[:, None] > r[None, :]).astype(F32)


def cmp_to_slc_matrix(n_cmp, n_slc):
    r, c = SLC_LEN // CMP_STRIDE, CMP_LEN // CMP_STRIDE
    offs = (np.arange(r)[:, None] + np.arange(c)[None, :]).reshape(-1)
    j = np.arange(n_slc)
    idx = r * j[None, :] + offs[:, None]
    jj = np.broadcast_to(j[None, :], idx.shape)
    ok = idx < n_cmp
    m = np.zeros((n_cmp, n_slc), np.float32)
    np.add.at(m, (idx[ok], jj[ok]), 1.0)
    return m


def compress_blocks(kv, pos, w1, w2):
    s = kv.shape[1]
    n_cmp = (s - CMP_LEN) // CMP_STRIDE + 1
    idx = np.arange(n_cmp)[:, None] * CMP_STRIDE + np.arange(CMP_LEN)[None, :]
    blk = kv[:, idx] + pos[None, None, :, None, :]
    h = jax.nn.gelu(jnp.einsum('bnlgd,lde->bgne', blk, w1))
    return jnp.einsum('bgne,ef->bgnf', h, w2)


def nsa_mixer(q, k_cmp, v_cmp, k_slc, v_slc, k_win, v_win, gates):
    b, g, hg, s, dh = q.shape
    n_cmp = k_cmp.shape[2]
    n_slc = s // SLC_LEN
    k_sel = min(SLC_TOPK, n_slc)
    scale = dh ** -0.5
    slopes = alibi_slopes(g * hg).reshape(1, g, hg, 1, 1)
    cmp_end = jnp.arange(n_cmp) * CMP_STRIDE + (CMP_LEN - 1)
    imp_map = jnp.asarray(cmp_to_slc_matrix(n_cmp, n_slc))
    blk_start = jnp.arange(n_slc) * SLC_LEN
    jj = jnp.arange(n_slc)
    kb = k_slc.reshape(b, g, n_slc, SLC_LEN, dh)
    vb = v_slc.reshape(b, g, n_slc, SLC_LEN, dh)
    pad = ((0, 0), (0, 0), (WINDOW, 0), (0, 0))
    kw_pad = jnp.pad(k_win, pad)
    vw_pad = jnp.pad(v_win, pad)
    b_ix = jnp.arange(b)[:, None, None, None]
    g_ix = jnp.arange(g)[None, :, None, None]

    def block(qb):
        t0 = qb * Q_BLOCK
        t = t0 + jnp.arange(Q_BLOCK)
        qt = lax.dynamic_slice_in_dim(q, t0, Q_BLOCK, axis=3)
        gt = jax.nn.sigmoid(lax.dynamic_slice_in_dim(gates, t0, Q_BLOCK, axis=3).astype(F32))
        dist_c = (t[:, None] - cmp_end[None, :]).astype(F32)
        mask_c = dist_c >= 0
        s_c = jnp.einsum('bghqd,bgnd->bghqn', qt, k_cmp).astype(F32) * scale
        p_c = masked_softmax(s_c - slopes * dist_c, mask_c) * mask_c
        o_c = jnp.einsum('bghqn,bgnd->bghqd', p_c, v_cmp)
        imp = jnp.einsum('bgqn,nj->bgqj', jnp.sum(p_c, axis=2), imp_map)
        cur = t // SLC_LEN
        forced = (jj[None, :] == 0) | (jj[None, :] == cur[:, None]) | (jj[None, :] == cur[:, None] - 1)
        valid = blk_start[None, :] <= t[:, None]
        score = jnp.where(valid, imp + SEL_BONUS * forced, -jnp.inf)
        _, sel = lax.top_k(score, k_sel)
        ks = kb[b_ix, g_ix, sel].reshape(b, g, Q_BLOCK, k_sel * SLC_LEN, dh)
        vs = vb[b_ix, g_ix, sel].reshape(b, g, Q_BLOCK, k_sel * SLC_LEN, dh)
        tok = (sel[..., None] * SLC_LEN + jnp.arange(SLC_LEN)).reshape(b, g, Q_BLOCK, k_sel * SLC_LEN)
        dist_s = (t[None, None, :, None] - tok).astype(F32)[:, :, None]
        s_s = jnp.einsum('bghqd,bgqkd->bghqk', qt, ks).astype(F32) * scale
        p_s = masked_softmax(s_s - slopes * dist_s, dist_s >= 0)
        o_s = jnp.einsum('bghqk,bgqkd->bghqd', p_s, vs)
        kw = lax.dynamic_slice_in_dim(kw_pad, t0, WINDOW + Q_BLOCK, axis=2)
        vw = lax.dynamic_slice_in_dim(vw_pad, t0, WINDOW + Q_BLOCK, axis=2)
        kpos = t0 - WINDOW + jnp.arange(WINDOW + Q_BLOCK)
        dist_w = (t[:, None] - kpos[None, :]).astype(F32)
        mask_w = (dist_w >= 0) & (dist_w < WINDOW) & (kpos[None, :] >= 0)
        s_w = jnp.einsum('bghqd,bgkd->bghqk', qt, kw).astype(F32) * scale
        p_w = masked_softmax(s_w - slopes * dist_w, mask_w)
        o_w = jnp.einsum('bghqk,bgkd->bghqd', p_w, vw)
        o = gt[..., 0:1] * o_c + gt[..., 1:2] * o_s + gt[..., 2:3] * o_w
        return o.astype(q.dtype)

    out = lax.map(block, jnp.arange(s // Q_BLOCK))
    return out.transpose(1, 0, 4, 2, 3, 5).reshape(b, s, g * hg * dh)


def stick_breaking_mixer(q, k, v):
    b, h, s, dh = q.shape
    q = q * (dh ** -0.5)
    tri = strict_lower(Q_BLOCK)
    outs = []
    for i in range(s // Q_BLOCK):
        t0 = i * Q_BLOCK
        n_kb = i + 1
        kl = t0 + Q_BLOCK
        t = t0 + jnp.arange(Q_BLOCK)
        z = jnp.einsum('bhqd,bhkd->bhqk', q[:, :, t0:kl], k[:, :, :kl]).astype(F32)
        mask = jnp.arange(kl)[None, :] < t[:, None]
        sp = jax.nn.softplus(z)
        log_1mb = jnp.where(mask, -sp, 0.0).reshape(b, h, Q_BLOCK, n_kb, Q_BLOCK)
        within = jnp.einsum('bhqnj,jk->bhqnk', log_1mb, tri)
        across = jnp.einsum('bhqm,mn->bhqn', jnp.sum(log_1mb, axis=-1), strict_lower(n_kb))
        suffix = (within + across[..., None]).reshape(b, h, Q_BLOCK, kl)
        a = jnp.where(mask, jnp.exp(z - sp + suffix), 0.0)
        outs.append(jnp.einsum('bhqk,bhkd->bhqd', a, v[:, :, :kl]).astype(v.dtype))
    out = jnp.concatenate(outs, axis=2)
    return out.transpose(0, 2, 1, 3).reshape(b, s, h * dh)


def forgetting_mixer(q, k, v, log_f):
    b, h, s, dh = q.shape
    q = q * (dh ** -0.5)
    c = jnp.cumsum(log_f, axis=-1)
    outs = []
    for i in range(s // Q_BLOCK):
        t0 = i * Q_BLOCK
        kl = t0 + Q_BLOCK
        t = t0 + jnp.arange(Q_BLOCK)
        logits = jnp.einsum('bhqd,bhkd->bhqk', q[:, :, t0:kl], k[:, :, :kl]).astype(F32) + c[:, :, t0:kl, None] - c[:, :, None, :kl]
        p = masked_softmax(logits, jnp.arange(kl)[None, :] <= t[:, None])
        outs.append(jnp.einsum('bhqk,bhkd->bhqd', p, v[:, :, :kl]).astype(v.dtype))
    out = jnp.concatenate(outs, axis=2)
    return out.transpose(0, 2, 1, 3).reshape(b, s, h * dh)


def hybrid_mixer(x, w_in, cmp_pos_k, cmp_w1_k, cmp_w2_k, cmp_pos_v, cmp_w1_v, cmp_w2_v, fox_forget_bias, norm_nsa, norm_sb, norm_fox, w_out):
    b, s, _ = x.shape
    g, hg = N_NSA_KV, NSA_HPG
    parts = jnp.split(x @ w_in, np.cumsum(PROJ_SIZES)[:-1].tolist(), axis=-1)
    (nq, ck, cv, sk, sv, wk, wv, ng, sbq, sbk, sbv, fq, fk, fv, ff) = parts
    group_kv = lambda t: t.reshape(b, s, g, HEAD_DIM)
    to_bgsd = lambda t: t.reshape(b, s, g, HEAD_DIM).transpose(0, 2, 1, 3)
    heads = lambda t, n: t.reshape(b, s, n, HEAD_DIM).transpose(0, 2, 1, 3)
    q_nsa = nq.reshape(b, s, g, hg, HEAD_DIM).transpose(0, 2, 3, 1, 4)
    k_cmp = compress_blocks(group_kv(ck), cmp_pos_k, cmp_w1_k, cmp_w2_k)
    v_cmp = compress_blocks(group_kv(cv), cmp_pos_v, cmp_w1_v, cmp_w2_v)
    gates = ng.reshape(b, s, g, hg, 3).transpose(0, 2, 3, 1, 4)
    o_nsa = nsa_mixer(q_nsa, k_cmp, v_cmp, to_bgsd(sk), to_bgsd(sv), to_bgsd(wk), to_bgsd(wv), gates)
    o_sb = stick_breaking_mixer(heads(sbq, N_SB_HEADS), heads(sbk, N_SB_HEADS), heads(sbv, N_SB_HEADS))
    log_f = jax.nn.log_sigmoid((ff + fox_forget_bias).astype(F32)).transpose(0, 2, 1)
    o_fox = forgetting_mixer(heads(fq, N_FOX_HEADS), heads(fk, N_FOX_HEADS), heads(fv, N_FOX_HEADS), log_f)
    y = jnp.concatenate([rms_norm(o_nsa, norm_nsa), rms_norm(o_sb, norm_sb), rms_norm(o_fox, norm_fox)], axis=-1)
    return y @ w_out


def hierarchical_moe(x, rg_w, rg_b, re_w, re_b, w_gate, w_up, w_down):
    b, s, d = x.shape
    xt = x.reshape(b * s, d)
    n_tok = b * s
    p_grp = jax.nn.softmax((xt @ rg_w + rg_b).astype(F32), axis=-1)
    g_sel = jnp.argmax(p_grp, axis=-1)
    g_w = jnp.max(p_grp, axis=-1)
    e_logits = (xt @ re_w + re_b).astype(F32).reshape(n_tok, N_GROUPS, EXPERTS_PER_GROUP)
    e_logits = e_logits[jnp.arange(n_tok), g_sel]
    top_v, top_i = lax.top_k(e_logits, INNER_TOPK)
    w = jax.nn.softmax(top_v, axis=-1) * g_w[:, None]
    expert_id = g_sel[:, None] * EXPERTS_PER_GROUP + top_i
    gate = jnp.sum(jax.nn.one_hot(expert_id, N_EXPERTS, dtype=F32) * w[..., None], axis=1)

    def expert_step(acc, inp):
        wg, wu, wd, gcol = inp
        hid = jax.nn.silu(xt @ wg) * (xt @ wu)
        return acc + (hid * gcol[:, None]) @ wd, None

    y, _ = lax.scan(expert_step, jnp.zeros_like(xt), (w_gate, w_up, w_down, gate.T.astype(x.dtype)))
    return y.reshape(b, s, d)


def setup_inputs(seed: int = 0) -> dict:
    key = jax.random.key(seed)
    ks = jax.random.split(key, 26)
    L, D, dh = DEPTH, D_MODEL, HEAD_DIM
    nrm = lambda k, shape, sc: jax.random.normal(k, shape, F32) * sc
    col_scale = np.concatenate([np.full(n, BETA_INIT if i in V_PARTS else 1.0, np.float32) for i, n in enumerate(PROJ_SIZES)])
    return {
        'x': nrm(ks[0], (BATCH, SEQ, D), 1.0),
        'w_in': nrm(ks[1], (L, D, PROJ_DIM), D ** -0.5) * jnp.asarray(col_scale),
        'cmp_pos_k': nrm(ks[2], (L, CMP_LEN, dh), 0.1),
        'cmp_w1_k': nrm(ks[3], (L, CMP_LEN, dh, dh), (CMP_LEN * dh) ** -0.5),
        'cmp_w2_k': nrm(ks[4], (L, dh, dh), dh ** -0.5),
        'cmp_pos_v': nrm(ks[5], (L, CMP_LEN, dh), 0.1),
        'cmp_w1_v': nrm(ks[6], (L, CMP_LEN, dh, dh), (CMP_LEN * dh) ** -0.5),
        'cmp_w2_v': nrm(ks[7], (L, dh, dh), dh ** -0.5),
        'fox_forget_bias': jnp.linspace(1.0, 6.0, N_FOX_HEADS, dtype=F32)[None, :] + nrm(ks[8], (L, N_FOX_HEADS), 0.1),
        'norm_nsa': 1.0 + nrm(ks[9], (L, NSA_Q_W), 0.01),
        'norm_sb': 1.0 + nrm(ks[10], (L, SB_W), 0.01),
        'norm_fox': 1.0 + nrm(ks[11], (L, FOX_W), 0.01),
        'w_out': nrm(ks[12], (L, MIX_WIDTH, D), MIX_WIDTH ** -0.5 * BETA_INIT),
        'ln1_g': 1.0 + nrm(ks[13], (L, D), 0.01),
        'ln1_b': nrm(ks[14], (L, D), 0.01),
        'router_group_w': nrm(ks[15], (L, D, N_GROUPS), D ** -0.5),
        'router_group_b': nrm(ks[16], (L, N_GROUPS), 0.01),
        'router_expert_w': nrm(ks[17], (L, D, N_EXPERTS), D ** -0.5),
        'router_expert_b': nrm(ks[18], (L, N_EXPERTS), 0.01),
        'expert_w_gate': nrm(ks[19], (L, N_EXPERTS, D, D_FF_EXPERT), D ** -0.5 * BETA_INIT),
        'expert_w_up': nrm(ks[20], (L, N_EXPERTS, D, D_FF_EXPERT), D ** -0.5 * BETA_INIT),
        'expert_w_down': nrm(ks[21], (L, N_EXPERTS, D_FF_EXPERT, D), D_FF_EXPERT ** -0.5 * BETA_INIT),
        'ln2_g': 1.0 + nrm(ks[22], (L, D), 0.01),
        'ln2_b': nrm(ks[23], (L, D), 0.01),
    }


def reference(x, w_in, cmp_pos_k, cmp_w1_k, cmp_w2_k, cmp_pos_v, cmp_w1_v, cmp_w2_v, fox_forget_bias, norm_nsa, norm_sb, norm_fox, w_out, ln1_g, ln1_b, router_group_w, router_group_b, router_expert_w, router_expert_b, expert_w_gate, expert_w_up, expert_w_down, ln2_g, ln2_b):
    h = x
    for l in range(DEPTH):
        mix = hybrid_mixer(h, w_in[l], cmp_pos_k[l], cmp_w1_k[l], cmp_w2_k[l], cmp_pos_v[l], cmp_w1_v[l], cmp_w2_v[l], fox_forget_bias[l], norm_nsa[l], norm_sb[l], norm_fox[l], w_out[l])
        h = layer_norm(ALPHA * h + mix, ln1_g[l], ln1_b[l])
        ffn = hierarchical_moe(h, router_group_w[l], router_group_b[l], router_expert_w[l], router_expert_b[l], expert_w_gate[l], expert_w_up[l], expert_w_down[l])
        h = layer_norm(ALPHA * h + ffn, ln2_g[l], ln2_b[l])
    return h
```

```python
import contextlib
import numpy as np
import concourse.bass as bass
import concourse.mybir as mybir

F32 = mybir.dt.float32
BF16 = mybir.dt.bfloat16
AF = mybir.ActivationFunctionType
ALU = mybir.AluOpType
AX = mybir.AxisListType

CENGS = ("pe", "act", "dve", "pool", "sp")


class Prog:
    def __init__(self, nc):
        self.nc = nc
        self.ops = {e: [] for e in CENGS}
        self.cnt = {}
        self.waited = {e: {} for e in CENGS}
        self.lastw = {}
        self.readers = {}
        self.sems = {}
        self.nops = 0

    def _issue(self, eng, chan, fn, reads, writes, accum):
        deps = {}
        def add(d):
            if d is not None and deps.get(d[0], 0) < d[1]:
                deps[d[0]] = d[1]
        for k in reads:
            add(self.lastw.get(k))
        for k in writes:
            lw = self.lastw.get(k)
            if not (accum and lw is not None and lw[0] == "pe" and eng == "pe"):
                add(lw)
            for x, n in self.readers.get(k, {}).items():
                add((x, n))
        waits = []
        wd = self.waited[eng]
        for x, n in deps.items():
            if wd.get(x, 0) < n:
                wd[x] = n
                waits.append((x, n))
        self.cnt[chan] = self.cnt.get(chan, 0) + 1
        idx = self.cnt[chan]
        self.ops[eng].append((waits, fn, chan))
        for k in reads:
            self.readers.setdefault(k, {})[chan] = idx
        for k in writes:
            self.lastw[k] = (chan, idx)
            self.readers[k] = {}
        self.nops += 1
        return idx

    def op(self, eng, fn, reads=(), writes=(), accum=False):
        return self._issue(eng, eng, fn, reads, writes, accum)

    def dma(self, fn, slot, reads=(), writes=(), eng="sp"):
        return self._issue(eng, "dma:" + str(slot), fn, reads, writes, False)

    def final_wait(self, eng="sp"):
        waits = [(c, n) for c, n in self.cnt.items()]
        self.ops[eng].append((waits, None, None))

    def emit(self):
        nc = self.nc
        with contextlib.ExitStack() as st:
            for c in self.cnt:
                self.sems[c] = st.enter_context(nc.semaphore("s_" + c.replace(":", "_")))
            block = st.enter_context(nc.Block())
            dec = {"pe": block.tensor, "act": block.scalar, "dve": block.vector,
                   "pool": block.gpsimd, "sp": block.sync}
            for e in CENGS:
                ops = self.ops[e]
                if not ops:
                    continue
                def body(eng, e=e, ops=ops):
                    for waits, fn, chan in ops:
                        for (x, m) in waits:
                            eng.wait_ge(self.sems[x], m * (16 if x.startswith("dma:") else 1))
                        if fn is None:
                            continue
                        ins = fn(eng)
                        ins.then_inc(self.sems[chan], 16 if chan.startswith("dma:") else 1)
                dec[e](body)

from concourse.bass_utils import run_bass_kernel_spmd
import ml_dtypes

NEG = -30000.0
D_MODEL = 2048
NCH = 16


def _bf(a):
    return np.asarray(a, np.float32).astype(ml_dtypes.bfloat16)


class KB:
    def __init__(self, nb=8):
        self.nc = bass.Bass("TRN2", target_bir_lowering=False)
        self.p = Prog(self.nc)
        self.psum = self.nc.alloc_psum_tensor("psum", [128, nb * 512], F32)
        self._cast_rr = 0

    def din(self, name, shape, dt=F32):
        return self.nc.dram_tensor(name, list(shape), dt, kind="ExternalInput").ap()

    def dout(self, name, shape, dt=F32):
        return self.nc.dram_tensor(name, list(shape), dt, kind="ExternalOutput").ap()

    def dscr(self, name, shape, dt=F32):
        return self.nc.dram_tensor(name, list(shape), dt, kind="Internal").ap()

    def sb(self, name, shape, dt=F32):
        return self.nc.alloc_sbuf_tensor("sb_" + name, list(shape), dt)

    def bank(self, b):
        return self.psum[:, b * 512:(b + 1) * 512]

    def dma(self, out, in_, slot, reads=(), writes=(), eng="sp"):
        self.p.dma(lambda e: e.dma_start(out=out, in_=in_), slot, reads, writes, eng)

    def mm(self, out, lhsT, rhs, start, stop, reads, writes, sgc=False):
        self.p.op("pe", lambda e: e.matmul(out, lhsT, rhs, start=start, stop=stop, skip_group_check=sgc),
                  reads, writes, accum=not start)

    def tr(self, out, in_, ident, reads, writes):
        self.p.op("pe", lambda e: e.transpose(out, in_, ident), reads, writes)

    def act(self, out, in_, func, reads, writes, bias=0.0, scale=1.0):
        self.p.op("act", lambda e: e.activation(out, in_, func, bias=bias, scale=scale), reads, writes)

    def ts(self, eng, out, in0, s1, s2, op0, op1=None, reads=(), writes=()):
        if op1 is None:
            self.p.op(eng, lambda e: e.tensor_scalar(out, in0, s1, s2, op0), reads, writes)
        else:
            self.p.op(eng, lambda e: e.tensor_scalar(out, in0, s1, s2, op0, op1), reads, writes)

    def tt(self, eng, out, in0, in1, op, reads=(), writes=()):
        self.p.op(eng, lambda e: e.tensor_tensor(out, in0, in1, op), reads, writes)

    def stt(self, eng, out, in0, scalar, in1, op0, op1, reads=(), writes=()):
        self.p.op(eng, lambda e: e.scalar_tensor_tensor(out, in0, scalar, in1, op0, op1), reads, writes)

    def copy(self, eng, out, in_, reads=(), writes=()):
        if eng == "act":
            self.p.op(eng, lambda e: e.copy(out, in_), reads, writes)
        else:
            self.p.op(eng, lambda e: e.tensor_copy(out, in_), reads, writes)

    def memset(self, eng, ap, val, writes=()):
        self.p.op(eng, lambda e: e.memset(ap, val), (), writes)

    def reduce(self, eng, out, in_, op, reads=(), writes=()):
        self.p.op(eng, lambda e: e.tensor_reduce(out, in_, AX.X, op), reads, writes)

    def finish(self):
        self.p.final_wait("sp")
        self.p.emit()
        return self.nc


def stream_project(kb, xT, S, TT, wbf, wkey, consume, nbuf=2, NST=4):
    NT = S // TT
    stg = [kb.sb(f"xstg{i}", [128, TT], F32) for i in range(NST)]
    hT = [kb.sb(f"hT{i}", [128, NCH, TT], BF16) for i in range(nbuf)]
    n = 0
    for T in range(NT):
        hb = hT[T % nbuf]
        hkey = ("hT", T % nbuf)
        for c in range(NCH):
            s = n % NST
            kb.dma(stg[s][:, :], xT[c * 128:(c + 1) * 128, T * TT:(T + 1) * TT], f"xs{s}",
                   writes=[("xstg", s)])
            eng = ("dve", "pool")[n % 2]
            kb.copy(eng, hb[:, c, :], stg[s][:, :], reads=[("xstg", s)], writes=[(hkey, c)])
            n += 1
        consume(T, hb, [(hkey, c) for c in range(NCH)])


def fox_consts():
    i = np.arange(128)
    ident = np.eye(128, dtype=np.float32)
    tri_inc = (i[:, None] <= i[None, :]).astype(np.float32)
    ones = np.ones((128, 128), np.float32)
    U = (i[:, None] < i[None, :]).astype(np.float32)
    cf = np.concatenate([ident, tri_inc, ones, U], axis=1)
    q = np.arange(512)
    masks = [np.where(128 * j + i[:, None] <= q[None, :], 0.0, NEG) for j in range(4)]
    cb = _bf(np.concatenate(masks, axis=1))
    return cf, cb


def build_fox(S):
    kb = KB()
    nc = kb.nc
    NT, NK = S // 512, S // 128
    xT = kb.din("xT", [D_MODEL, S])
    w = kb.din("w", [D_MODEL, 193])
    fb = kb.din("fb", [128, 1])
    cf_d = kb.din("cf", [128, 512])
    cb_d = kb.din("cb", [128, 2048], BF16)
    oT = kb.dout("oT", [65, S])
    scr = kb.dscr("scr", [1, S], BF16)

    cf = kb.sb("cf", [128, 512]); cb = kb.sb("cb", [128, 2048], BF16)
    kb.dma(cf[:, :], cf_d[:, :], "c0", writes=["cf"])
    kb.dma(cb[:, :], cb_d[:, :], "c1", writes=["cb"])
    ident, tri_inc, ones, U = (cf[:, 0:128], cf[:, 128:256], cf[:, 256:384], cf[:, 384:512])
    fbs = kb.sb("fbs", [128, 1]); nfb = kb.sb("nfb", [128, 1])
    kb.dma(fbs[:, :], fb[:, :], "c2", writes=["fbs"])
    kb.ts("dve", nfb[:, :], fbs[:, :], -1.0, None, ALU.mult, reads=["fbs"], writes=["nfb"])

    wf = kb.sb("wf", [128, NCH, 193]); wbf = kb.sb("wbf", [128, NCH, 196], BF16)
    kb.dma(wf[:, :, :], w.rearrange("(c p) n -> p c n", p=128), "c3", writes=["wf"])
    kb.memset("dve", wbf[:, :, :], 0.0, writes=["wbf"])
    kb.copy("dve", wbf[:, :, 0:193], wf[:, :, :], reads=["wf"], writes=["wbf"])

    Qa = kb.sb("Qa", [65, S], BF16); Ka = kb.sb("Ka", [65, S], BF16)
    Va = kb.sb("Va", [128, NK, 65], BF16)
    FF2 = kb.sb("FF2", [128, NK])
    kb.memset("pool", Ka[64:65, :], 1.0, writes=["Ka1"])
    kb.memset("pool", Va[:, :, 64:65], 1.0, writes=["Va1"])

    def consume(T, hb, hkeys):
        ts_ = slice(T * 512, (T + 1) * 512)
        import os
        CS = int(os.environ.get("CS", "9"))
        if CS == 0:
            return
        b = kb.bank(0)
        for c in range(NCH):
            kb.mm(b[0:64, :], wbf[:, c, 0:64], hb[:, c, :], c == 0, c == NCH - 1, ["wbf", hkeys[c]], ["pb0"])
        kb.ts("dve", Qa[0:64, ts_], b[0:64, :], 0.125, None, ALU.mult, reads=["pb0"], writes=[("Qa", T)])
        if CS == 1:
            return
        b = kb.bank(1)
        for c in range(NCH):
            kb.mm(b[0:64, :], wbf[:, c, 64:128], hb[:, c, :], c == 0, c == NCH - 1, ["wbf", hkeys[c]], ["pb1"])
        kb.copy("act", Ka[0:64, ts_], b[0:64, :], reads=["pb1"], writes=[("Ka", T)])
        if CS == 2:
            return
        for j in range(4):
            kt = 4 * T + j
            bi = j % 2
            b = kb.bank(bi)
            for c in range(NCH):
                kb.mm(b[:, 0:66], hb[:, c, j * 128:(j + 1) * 128], wbf[:, c, 128:194], c == 0, c == NCH - 1,
                      ["wbf", hkeys[c]], [f"pb{bi}"])
            if CS == 3:
                continue
            kb.copy("dve", Va[:, kt, 0:64], b[:, 0:64], reads=[f"pb{bi}"], writes=[("Va", kt)])
            if CS == 4:
                continue
            kb.copy("dve", FF2[:, kt:kt + 1], b[:, 64:65], reads=[f"pb{bi}"], writes=["FF2"])

    import os
    STOP = int(os.environ.get("STOP", "99"))
    if STOP == 0:
        return kb.finish()
    stream_project(kb, xT, S, 512, wbf, "wbf", consume)
    if STOP == 1:
        return kb.finish()

    E = kb.sb("E", [128, NK]); SP2 = kb.sb("SP2", [128, NK])
    kb.act(E[:, :], FF2[:, :], AF.Exp, ["FF2", "nfb"], ["E"], bias=nfb[:, 0:1], scale=-1.0)
    kb.act(SP2[:, :], E[:, :], AF.Ln, ["E"], ["SP2"], bias=1.0, scale=1.0)
    b0 = kb.bank(0)
    kb.tr(b0[0:NK, 0:128], SP2[:, :], ident, ["SP2", "cf"], ["pb0"])
    tot = kb.sb("tot", [128, 1]); TotB = kb.sb("TotB", [128, 128])
    kb.reduce("dve", tot[0:NK, :], b0[0:NK, 0:128], ALU.add, reads=["pb0"], writes=["tot"])
    kb.ts("dve", TotB[0:NK, :], ones[0:NK, :], tot[0:NK, 0:1], None, ALU.mult, reads=["cf", "tot"], writes=["TotB"])
    b1 = kb.bank(1)
    kb.mm(b1[:, 0:NK], tri_inc, SP2[:, :], True, False, ["cf", "SP2"], ["pb1"])
    kb.mm(b1[:, 0:NK], TotB[0:NK, :], U[0:NK, 0:NK], False, True, ["TotB", "cf"], ["pb1"])
    D2 = kb.sb("D2", [128, NK])
    kb.copy("dve", D2[:, :], b1[:, 0:NK], reads=["pb1"], writes=["D2"])
    kb.tr(b0[0:NK, 0:128], D2[:, :], ident, ["D2", "cf"], ["pb0"])
    Drow = kb.sb("Drow", [128, 128], BF16)
    kb.ts("dve", Drow[0:NK, :], b0[0:NK, 0:128], -1.0, None, ALU.mult, reads=["pb0"], writes=["Drow"])
    kb.dma(scr.rearrange("o (k p) -> (o k) p", p=128), Drow[0:NK, :], "c4", reads=["Drow"], writes=["scr"])
    kb.dma(Qa[64:65, :], scr[:, :], "c5", reads=["scr"], writes=["Qa1"])

    if STOP == 2:
        return kb.finish()
    NP = 3
    Pb = [kb.sb(f"P{i}", [128, 512], BF16) for i in range(NP)]
    tmp = [kb.sb(f"tmp{i}", [128, 512]) for i in range(2)]
    osb = [kb.sb(f"osb{i}", [65, 512]) for i in range(2)]
    pairs = [(T, kt) for T in range(NT) for kt in range(4 * T + 4)]
    allQ = [("Qa", T) for T in range(NT)]

    def stage1(i):
        T, kt = pairs[i]
        sb_i = 2 + i % 4
        ps = kb.bank(sb_i)
        kb.mm(ps, Ka[0:65, kt * 128:(kt + 1) * 128], Qa[0:65, T * 512:(T + 1) * 512], True, True,
              [("Ka", kt // 4), "Ka1", ("Qa", T), "Qa1"], [f"pb{sb_i}"])
        pk = ("P", i % NP)
        if kt >= 4 * T:
            j = kt - 4 * T
            tk = ("tmp", i % 2)
            kb.tt("dve", tmp[i % 2][:, :], ps, cb[:, j * 512:(j + 1) * 512], ALU.add, reads=[f"pb{sb_i}", "cb"], writes=[tk])
            kb.act(Pb[i % NP][:, :], tmp[i % 2][:, :], AF.Exp, [tk, "D2"], [pk], bias=D2[:, kt:kt + 1])
        else:
            kb.act(Pb[i % NP][:, :], ps, AF.Exp, [f"pb{sb_i}", "D2"], [pk], bias=D2[:, kt:kt + 1])

    def stage2(i):
        T, kt = pairs[i]
        ob = 6 + T % 2
        po = kb.bank(ob)
        last = kt == 4 * T + 3
        kb.mm(po[0:65, :], Va[:, kt, 0:65], Pb[i % NP][:, :], kt == 0, last,
              [("Va", kt), "Va1", ("P", i % NP)], [f"pb{ob}"])
        if last:
            kb.copy("dve", osb[T % 2][:, :], po[0:65, :], reads=[f"pb{ob}"], writes=[("osb", T % 2)])
            kb.dma(oT[:, T * 512:(T + 1) * 512], osb[T % 2][:, :], f"o{T % 2}", reads=[("osb", T % 2)])

    n = len(pairs)
    LA = 2
    for i in range(n + LA):
        if i < n:
            stage1(i)
        if i >= LA:
            stage2(i - LA)
    return kb.finish()


def sb_consts():
    i = np.arange(128)
    tri = (i[:, None] >= i[None, :]).astype(np.float32)
    ntri = 1.0 - tri
    q = np.arange(512)
    masks = [np.where(128 * j + i[:, None] < q[None, :], 1.0, 0.0) for j in range(4)]
    cb = _bf(np.concatenate([tri, ntri] + masks, axis=1))
    return cb


def build_sb(S):
    kb = KB()
    NT, NK = S // 512, S // 128
    xT = kb.din("xT", [D_MODEL, S])
    w = kb.din("w", [D_MODEL, 192])
    cb_d = kb.din("cb", [128, 2304], BF16)
    oT = kb.dout("oT", [64, S])
    cb = kb.sb("cb", [128, 2304], BF16)
    kb.dma(cb[:, :], cb_d[:, :], "c1", writes=["cb"])
    tri, ntri = cb[:, 0:128], cb[:, 128:256]
    wf = kb.sb("wf", [128, NCH, 192]); wbf = kb.sb("wbf", [128, NCH, 192], BF16)
    kb.dma(wf[:, :, :], w.rearrange("(c p) n -> p c n", p=128), "c3", writes=["wf"])
    kb.copy("dve", wbf[:, :, :], wf[:, :, :], reads=["wf"], writes=["wbf"])
    Qa = kb.sb("Qa", [64, S], BF16); Ka = kb.sb("Ka", [64, S], BF16)
    Va = kb.sb("Va", [128, NK, 64], BF16)

    def consume(T, hb, hkeys):
        ts_ = slice(T * 512, (T + 1) * 512)
        b = kb.bank(0)
        for c in range(NCH):
            kb.mm(b[0:64, :], wbf[:, c, 0:64], hb[:, c, :], c == 0, c == NCH - 1, ["wbf", hkeys[c]], ["pb0"])
        kb.ts("dve", Qa[0:64, ts_], b[0:64, :], 0.125, None, ALU.mult, reads=["pb0"], writes=[("Qa", T)])
        b = kb.bank(1)
        for c in range(NCH):
            kb.mm(b[0:64, :], wbf[:, c, 64:128], hb[:, c, :], c == 0, c == NCH - 1, ["wbf", hkeys[c]], ["pb1"])
        kb.copy("dve", Ka[0:64, ts_], b[0:64, :], reads=["pb1"], writes=[("Ka", T)])
        for j in range(4):
            kt = 4 * T + j
            bi = j % 2
            b = kb.bank(bi)
            for c in range(NCH):
                kb.mm(b[:, 0:64], hb[:, c, j * 128:(j + 1) * 128], wbf[:, c, 128:192], c == 0, c == NCH - 1,
                      ["wbf", hkeys[c]], [f"pb{bi}"])
            kb.copy("dve", Va[:, kt, 0:64], b[:, 0:64], reads=[f"pb{bi}"], writes=[("Va", kt)])

    stream_project(kb, xT, S, 512, wbf, "wbf", consume)

    NB = 3
    E1 = [kb.sb(f"E1_{i}", [128, 512]) for i in range(NB)]
    SPb = [kb.sb(f"SP_{i}", [128, 512], BF16) for i in range(NB)]
    E2 = [kb.sb(f"E2_{i}", [128, 512]) for i in range(NB)]
    Ab = [kb.sb(f"A_{i}", [128, 512], BF16) for i in range(NB)]
    osb = [kb.sb(f"osb{i}", [64, 512]) for i in range(2)]
    pairs = [(T, kt) for T in range(NT) for kt in range(4 * T + 3, -1, -1)]
    n = len(pairs)

    def st1(i):
        T, kt = pairs[i]
        zb = 2 + i % 2
        ps = kb.bank(zb)
        s = i % NB
        kb.mm(ps, Ka[0:64, kt * 128:(kt + 1) * 128], Qa[0:64, T * 512:(T + 1) * 512], True, True,
              [("Ka", kt // 4), ("Qa", T)], [f"pb{zb}"])
        kb.act(E1[s][:, :], ps, AF.Exp, [f"pb{zb}"], [("E1", s)])
        if kt >= 4 * T:
            j = kt - 4 * T
            kb.tt("pool", E1[s][:, :], E1[s][:, :], cb[:, 256 + j * 512:256 + (j + 1) * 512], ALU.mult,
                  reads=[("E1", s), "cb"], writes=[("E1", s)])
        kb.act(SPb[s][:, :], E1[s][:, :], AF.Ln, [("E1", s)], [("SP", s)], bias=1.0)

    def st2(i):
        T, kt = pairs[i]
        lb = 4 + T % 2
        pl = kb.bank(lb)
        s = i % NB
        first = kt == 4 * T + 3
        kb.mm(pl, tri, SPb[s][:, :], first, True, ["cb", ("SP", s)], [f"pb{lb}"], sgc=True)
        kb.act(E2[s][:, :], pl, AF.Exp, [f"pb{lb}"], [("E2", s)], scale=-1.0)
        kb.tt("dve", Ab[s][:, :], E1[s][:, :], E2[s][:, :], ALU.mult, reads=[("E1", s), ("E2", s)], writes=[("A", s)])
        if kt > 0:
            kb.mm(pl, ntri, SPb[s][:, :], False, True, ["cb", ("SP", s)], [f"pb{lb}"], sgc=True)

    def st3(i):
        T, kt = pairs[i]
        ob = 6 + T % 2
        po = kb.bank(ob)
        s = i % NB
        first = kt == 4 * T + 3
        last = kt == 0
        kb.mm(po[0:64, :], Va[:, kt, 0:64], Ab[s][:, :], first, last, [("Va", kt), ("A", s)], [f"pb{ob}"])
        if last:
            kb.copy("dve", osb[T % 2][:, :], po[0:64, :], reads=[f"pb{ob}"], writes=[("osb", T % 2)])
            kb.dma(oT[:, T * 512:(T + 1) * 512], osb[T % 2][:, :], f"o{T % 2}", reads=[("osb", T % 2)])

    for i in range(n + 2):
        if i < n:
            st1(i)
        if 1 <= i <= n:
            st2(i - 1)
        if i >= 2:
            st3(i - 2)
    return kb.finish()


def nsa_slopes():
    return (2.0 ** (-8.0 * np.arange(1, 17, dtype=np.float32) / 16)).astype(np.float32)


def nsa_consts(S, core):
    g, hh = core // 2, core % 2
    sl = nsa_slopes()
    my = [4 * g + 2 * hh, 4 * g + 2 * hh + 1]
    h4 = my + [4 * g + 2 * (1 - hh), 4 * g + 2 * (1 - hh) + 1]
    NK = S // 128
    t = np.arange(S, dtype=np.float32)
    rrow = _bf(-sl[h4][:, None] * t[None, :])
    p = np.arange(128, dtype=np.float32)
    pos = 128.0 * np.arange(NK, dtype=np.float32)[None, :] + p[:, None]
    posb = np.stack([sl[h] * pos for h in my], axis=1).astype(np.float32)
    npos = 16.0 * (128.0 * np.arange(8, dtype=np.float32)[None, :] + p[:, None]) + 15.0
    cposb = np.stack([sl[h] * npos for h in h4], axis=1).astype(np.float32)
    cposb[0, :, 0] = NEG
    k = np.arange(128)[:, None]; q = np.arange(512)[None, :]
    causal = [np.where(128 * j + k <= q, 0.0, NEG) for j in range(4)]
    cmpm = [np.where(16 * k + 15 <= 512 * b + q, 0.0, NEG) for b in range(4)]
    winm = []
    for m in range(8):
        d = q + 512 - 128 * m - k
        winm.append(np.where((d >= 0) & (d < 512), 0.0, NEG))
    masks = _bf(np.concatenate(causal + cmpm + winm, axis=1))
    jr = np.arange(128)[:, None]; c = np.arange(64 * 128)[None, :]
    EB = _bf((jr == 2 * (c // 128) + (c % 128) // 64).astype(np.float32))
    Ml = np.zeros((128, 34), np.float32)
    for kk in range(128):
        if kk % 4 in (1, 2, 3):
            Ml[kk, kk // 4] = (1.0, 2.0, 1.0)[kk % 4 - 1]
    Ml[:, 32] = 1.0
    Ml = _bf(Ml)
    qq = np.arange(128)[:, None]; m_ = np.arange(510)[None, :]
    jp = m_ - 254; cur = qq // 64
    Frel = np.where(jp > cur, -1e30, np.where((jp == cur) | (jp == cur - 1), 1e6, 0.0)).astype(np.float32)
    identb = _bf(np.eye(128))
    return dict(rrow=rrow, posb=posb.reshape(128, -1), cposb=cposb.reshape(128, -1), masks=masks, EB=EB,
                Ml=Ml, Frel=Frel, identb=identb)


def build_nsa(S, core=None):
    myl = [0, 1]
    kb = KB(7)
    nc = kb.nc
    NT, NK = S // 512, S // 128
    NW = 646
    xT = kb.din("xT", [D_MODEL, S])
    w = kb.din("w", [D_MODEL, NW])
    w1 = kb.din("w1", [64, 2 * 32 * 64])
    w2 = kb.din("w2", [64, 128])
    pT = kb.din("pT", [64, 64])
    rrow_d = kb.din("rrow", [4, S], BF16)
    posb_d = kb.din("posb", [128, 2 * NK])
    cposb_d = kb.din("cposb", [128, 32])
    masks_d = kb.din("masks", [128, 16 * 512], BF16)
    EB_d = kb.din("EB", [128, 8192], BF16)
    Ml_d = kb.din("Ml", [128, 34], BF16)
    Frel_d = kb.din("Frel", [128, 510])
    identb_d = kb.din("identb", [128, 128], BF16)
    oc = kb.dout("oc", [2, 65, S]); osel = kb.dout("os", [2, 65, S]); ow = kb.dout("ow", [2, 65, S])
    gT = kb.dout("gT", [6, S])
    psb = nc.alloc_psum_tensor("psb", [128, 1024], BF16)

    def ld(name, d, shape, dt=F32):
        t_ = kb.sb(name, shape, dt)
        kb.dma(t_[:, :], d[:, :], "c_" + name, writes=[name])
        return t_
    posb = ld("posb", posb_d, [128, 2 * NK]); cposb = ld("cposb", cposb_d, [128, 32])
    masks = ld("masks", masks_d, [128, 16 * 512], BF16); EB = ld("EB", EB_d, [128, 8192], BF16)
    Ml = ld("Ml", Ml_d, [128, 34], BF16); Frel = ld("Frel", Frel_d, [128, 510])
    identb = ld("identb", identb_d, [128, 128], BF16)
    w2f = ld("w2f", w2, [64, 128]); pTf = ld("pTf", pT, [64, 64])
    w1b = kb.sb("w1b", [64, 4096], BF16); w2b = kb.sb("w2b", [64, 128], BF16); pTb = kb.sb("pTb", [64, 64], BF16)
    w1st = kb.sb("w1st", [64, 512])
    for i_ in range(8):
        kb.dma(w1st[:, :], w1[:, i_ * 512:(i_ + 1) * 512], "w1st", writes=["w1st"])
        kb.copy("dve", w1b[:, i_ * 512:(i_ + 1) * 512], w1st[:, :], reads=["w1st"], writes=["w1b"])
    kb.copy("dve", w2b[:, :], w2f[:, :], reads=["w2f"], writes=["w2b"])
    kb.copy("dve", pTb[:, :], pTf[:, :], reads=["pTf"], writes=["pTb"])
    cbias = kb.sb("cbias", [64, 2])
    for kv in range(2):
        b = kb.bank(0)
        for l in range(32):
            kb.mm(b[0:64, kv:kv + 1], w1b[:, kv * 2048 + l * 64: kv * 2048 + (l + 1) * 64],
                  pTb[:, kv * 32 + l: kv * 32 + l + 1], l == 0, l == 31, ["w1b", "pTb"], ["pb0"])
        kb.copy("dve", cbias[:, kv:kv + 1], b[0:64, kv:kv + 1], reads=["pb0"], writes=["cbias"])

    wbf = kb.sb("wbf", [128, NCH, NW], BF16)
    wst = [kb.sb(f"wst{i}", [128, NW]) for i in range(2)]
    for c in range(NCH):
        kb.dma(wst[c % 2][:, :], w[c * 128:(c + 1) * 128, :], f"ws{c % 2}", writes=[("wst", c % 2)])
        kb.copy(("dve", "pool")[c % 2], wbf[:, c, :], wst[c % 2][:, :], reads=[("wst", c % 2)], writes=["wbf"])

    Ska = kb.sb("Ska", [65, S], BF16); Sva = kb.sb("Sva", [128, NK, 65], BF16)
    Wka = kb.sb("Wka", [65, 1024], BF16); Wva = kb.sb("Wva", [128, 8, 65], BF16)
    Kca = kb.sb("Kca", [65, 1024], BF16); VcT = kb.sb("VcT", [64, 1024], BF16); Vca = kb.sb("Vca", [128, 8, 65], BF16)
    kb.memset("pool", Ska[64:65, :], 1.0, writes=["Ska1"]); kb.memset("pool", Wka[64:65, :], 1.0, writes=["Wka1"])
    kb.memset("pool", Sva[:, :, 64:65], 1.0, writes=["Sva1"]); kb.memset("pool", Wva[:, :, 64:65], 1.0, writes=["Wva1"])
    kb.memset("pool", Kca[:, :], 0.0, writes=["Kca"]); kb.memset("pool", Kca[64:65, :], 1.0, writes=["Kca"])
    kb.memset("pool", VcT[:, :], 0.0, writes=["VcT"]); kb.memset("pool", Vca[:, :, :], 0.0, writes=["Vca"])
    kb.memset("pool", Vca[:, :, 64:65], 1.0, writes=["Vca"])
    Qa = [kb.sb(f"Qa{h}", [65, 512], BF16) for h in range(4)]
    cbuf = [kb.sb(f"cbuf{i}", [64, 528], BF16) for i in range(2)]
    kb.memset("pool", cbuf[0][:, :], 0.0, writes=[("cbuf", 0)]); kb.memset("pool", cbuf[1][:, :], 0.0, writes=[("cbuf", 1)])
    gsb = kb.sb("gsb", [6, 512])
    sums = kb.sb("sums", [128, 4]); rs = kb.sb("rs", [128, 4])
    imp = kb.sb("imp", [128, 256]); score = kb.sb("score", [128, 256]); top8 = kb.sb("top8", [128, 8]); thr = kb.sb("thr", [128, 1])
    negsel = kb.sb("negsel", [128, 256], BF16); negselT = kb.sb("negselT", [128, 2, 512], BF16)
    NP = 3
    Pb = [kb.sb(f"P{i}", [128, 512], BF16) for i in range(NP)]
    tmp = [kb.sb(f"tmp{i}", [128, 512]) for i in range(2)]
    osb = [kb.sb(f"osb{i}", [65, 512]) for i in range(2)]
    cnt = {"i": 0, "o": 0}
    gx = {k_: kb.sb("gx_" + k_, [64, 32]) for k_ in ("xs", "x2", "u", "e")}
    hg = kb.sb("hg", [64, 32], BF16)

    def attn(T, items, Kc, kkey, Qh, qkey, Vc, vkey, out_dst, hook=None, ring=None):
        n = len(items)
        ob = 5 + cnt["o"] % 2
        oslot = cnt["o"] % 2
        cnt["o"] += 1
        po = kb.bank(ob)
        base = cnt["i"]
        cnt["i"] += n

        def s1(i):
            kt, bias, mask, extra = items[i]
            gi = base + i
            sbk = 2 + gi % 2
            ps = kb.bank(sbk)
            kc = kt if ring is None else kt % ring
            kb.mm(ps, Kc[0:65, kc * 128:(kc + 1) * 128], Qh[0:65, :], True, extra is None,
                  list(kkey(kt)) + [qkey], [f"pb{sbk}"])
            if extra is not None:
                kb.mm(ps, extra[0], extra[1], False, True, extra[2], [f"pb{sbk}"])
            pk = ("P", gi % NP)
            if mask is not None:
                tk = ("tmp", gi % 2)
                kb.tt("dve", tmp[gi % 2][:, :], ps, mask, ALU.add, reads=[f"pb{sbk}", "masks"], writes=[tk])
                kb.act(Pb[gi % NP][:, :], tmp[gi % 2][:, :], AF.Exp, [tk, "posb", "cposb"], [pk], bias=bias)
            else:
                kb.act(Pb[gi % NP][:, :], ps, AF.Exp, [f"pb{sbk}", "posb", "cposb"], [pk], bias=bias)

        def s2(i):
            kt = items[i][0]
            gi = base + i
            kc = kt if ring is None else kt % ring
            kb.mm(po[0:65, :], Vc[:, kc, 0:65], Pb[gi % NP][:, :], i == 0, i == n - 1,
                  list(vkey(kt)) + [("P", gi % NP)], [f"pb{ob}"])
            if hook is not None:
                hook(kt, Pb[gi % NP], ("P", gi % NP))
        for i in range(n + 2):
            if i < n:
                s1(i)
            if i >= 2:
                s2(i - 2)
        if out_dst is not None:
            kb.copy("dve", osb[oslot][:, :], po[0:65, :], reads=[f"pb{ob}"], writes=[("osb", oslot)])
            kb.dma(out_dst, osb[oslot][:, :], f"o{oslot}", reads=[("osb", oslot)])

    def consume(T, hb, hkeys):
        ts_ = slice(T * 512, (T + 1) * 512)
        for h in range(4):
            b = kb.bank(h % 2)
            for c in range(NCH):
                kb.mm(b[0:64, :], wbf[:, c, h * 64:(h + 1) * 64], hb[:, c, :], c == 0, c == NCH - 1,
                      ["wbf", hkeys[c]], [f"pb{h % 2}"])
            kb.ts("dve", Qa[h][0:64, :], b[0:64, :], 0.125, None, ALU.mult, reads=[f"pb{h % 2}"], writes=[("Qa", h)])
            kb.dma(Qa[h][64:65, :], rrow_d[h:h + 1, ts_], f"rr{h}", writes=[("Qa", h)])
        for kv in range(2):
            b = kb.bank(kv)
            kb.copy("pool", cbuf[kv][:, 0:16], cbuf[kv][:, 512:528], reads=[("cbuf", kv)], writes=[("cbuf", kv)])
            for c in range(NCH):
                kb.mm(b[0:64, :], wbf[:, c, 256 + kv * 64:256 + (kv + 1) * 64], hb[:, c, :], c == 0, c == NCH - 1,
                      ["wbf", hkeys[c]], [f"pb{kv}"])
            kb.copy("dve", cbuf[kv][:, 16:528], b[0:64, :], reads=[f"pb{kv}"], writes=[("cbuf", kv)])
        for i_, (dst, dkey) in enumerate(((Ska, "Ska"), (Wka, "Wka"))):
            b = kb.bank(i_)
            for c in range(NCH):
                kb.mm(b[0:64, :], wbf[:, c, 384 + i_ * 64:384 + (i_ + 1) * 64], hb[:, c, :], c == 0, c == NCH - 1,
                      ["wbf", hkeys[c]], [f"pb{i_}"])
            dsl = ts_ if i_ == 0 else slice((T % 2) * 512, (T % 2 + 1) * 512)
            kb.copy("dve", dst[0:64, dsl], b[0:64, :], reads=[f"pb{i_}"], writes=[(dkey, T)])
        b = kb.bank(0)
        for c in range(NCH):
            kb.mm(b[0:6, :], wbf[:, c, 640:646], hb[:, c, :], c == 0, c == NCH - 1, ["wbf", hkeys[c]], ["pb0"])
        kb.copy("dve", gsb[:, :], b[0:6, :], reads=["pb0"], writes=["gsb"])
        kb.dma(gT[:, ts_], gsb[:, :], "og", reads=["gsb"])
        for j in range(4):
            kt = 4 * T + j
            bi = j % 2
            b = kb.bank(bi)
            for c in range(NCH):
                kb.mm(b[:, 0:128], hb[:, c, j * 128:(j + 1) * 128], wbf[:, c, 512:640], c == 0, c == NCH - 1,
                      ["wbf", hkeys[c]], [f"pb{bi}"])
            kb.copy("dve", Sva[:, kt, 0:64], b[:, 0:64], reads=[f"pb{bi}"], writes=[("Sva", kt)])
            kb.copy("dve", Wva[:, kt % 8, 0:64], b[:, 64:128], reads=[f"pb{bi}"], writes=[("Wva", kt)])
        a_t = T // 4
        for kv in range(2):
            b = kb.bank(kv)
            cv = cbuf[kv][:, :].rearrange("p (n s) -> p n s", s=16)
            for l in range(32):
                rhs = cv[:, 0:32, l] if l < 16 else cv[:, 1:33, l - 16]
                kb.mm(b[0:64, 0:32], w1b[:, kv * 2048 + l * 64: kv * 2048 + (l + 1) * 64], rhs, l == 0, l == 31,
                      ["w1b", ("cbuf", kv)], [f"pb{kv}"])
            xs, x2, u, e = gx["xs"], gx["x2"], gx["u"], gx["e"]
            kb.ts("dve", xs[:, :], b[0:64, 0:32], cbias[:, kv:kv + 1], None, ALU.add, reads=[f"pb{kv}", "cbias"], writes=["gxs"])
            kb.tt("dve", x2[:, :], xs[:, :], xs[:, :], ALU.mult, reads=["gxs"], writes=["gx2"])
            kb.ts("dve", u[:, :], x2[:, :], 0.044715, 1.0, ALU.mult, ALU.add, reads=["gx2"], writes=["gu"])
            kb.tt("dve", u[:, :], u[:, :], xs[:, :], ALU.mult, reads=["gu", "gxs"], writes=["gu"])
            kb.act(e[:, :], u[:, :], AF.Exp, ["gu"], ["ge"], scale=-1.5957691216057308)
            kb.ts("dve", e[:, :], e[:, :], 1.0, None, ALU.add, reads=["ge"], writes=["ge"])
            kb.p.op("dve", lambda en, e=e: en.reciprocal(e[:, :], e[:, :]), ["ge"], ["ge"])
            kb.tt("dve", hg[:, :], xs[:, :], e[:, :], ALU.mult, reads=["gxs", "ge"], writes=["hg"])
            b2 = kb.bank(kv)
            kb.mm(b2[0:64, 64:96], w2b[:, kv * 64:(kv + 1) * 64], hg[:, :], True, True, ["w2b", "hg"], [f"pb{kv}"])
            if kv == 0:
                kb.copy("dve", Kca[0:64, 32 * T:32 * T + 32], b2[0:64, 64:96], reads=[f"pb{kv}"], writes=["Kca"])
            else:
                kb.copy("dve", VcT[0:64, 32 * T:32 * T + 32], b2[0:64, 64:96], reads=[f"pb{kv}"], writes=["VcT"])
                kb.tr(psb[:, 0:64], VcT[0:64, a_t * 128:(a_t + 1) * 128], identb[0:64, 0:64], ["VcT", "identb"], ["psb"])
                kb.copy("dve", Vca[:, a_t, 0:64], psb[:, 0:64], reads=["psb"], writes=["Vca"])
        bsel = T % 4
        for h in range(4):
            items = []
            for a in range(a_t + 1):
                mask = masks[:, (4 + bsel) * 512:(5 + bsel) * 512] if a == a_t else None
                items.append((a, cposb[:, h * 8 + a:h * 8 + a + 1], mask, None))

            def hook(a, P, pkey, h=h):
                bi = kb.bank(4)
                for qb in range(4):
                    kb.mm(bi[:, qb * 34:qb * 34 + 34], P[:, qb * 128:(qb + 1) * 128], Ml[:, 0:34], True, True,
                          [pkey, "Ml"], ["pb4"])
                for qb in range(4):
                    kb.copy("dve", IMPx[:, qb, a * 33:(a + 1) * 33], bi[:, qb * 34:qb * 34 + 33],
                            reads=["pb4"], writes=["IMPx"])
            dst = oc[myl.index(h), :, ts_] if h in myl else None
            attn(T, items, Kca, lambda a: ["Kca"], Qa[h], ("Qa", h), Vca, lambda a: ["Vca"], dst, hook)
            for qb in range(4):
                v = IMPx[:, qb, :].rearrange("p (a c) -> p a c", c=33)
                kb.reduce("dve", sums[:, qb:qb + 1], v[:, :, 32], ALU.add, reads=["IMPx"], writes=["sums"])
            kb.ts("dve", sums[:, :], sums[:, :], 1e-30, None, ALU.max, reads=["sums"], writes=["sums"])
            kb.p.op("dve", lambda en: en.reciprocal(rs[:, :], sums[:, :]), ["sums"], ["rs"])
            for qb in range(4):
                v = IMPx[:, qb, :].rearrange("p (a c) -> p a c", c=33)
                iv = impacc[:, qb, :].rearrange("p (a c) -> p a c", c=32)
                if h == 0:
                    kb.ts("dve", iv, v[:, :, 0:32], rs[:, qb:qb + 1], None, ALU.mult, reads=["IMPx", "rs"], writes=["impacc"])
                else:
                    kb.stt("dve", iv, v[:, :, 0:32], rs[:, qb:qb + 1], iv, ALU.mult, ALU.add,
                           reads=["IMPx", "rs", "impacc"], writes=["impacc"])
        for qb in range(4):
            B = 4 * T + qb
            kb.tt("dve", score[:, :], impacc[:, qb, :], Frel[:, 254 - 2 * B:510 - 2 * B], ALU.add, reads=["impacc", "Frel"], writes=["score"])
            kb.ts("dve", score[:, 0:1], score[:, 0:1], 1e6, None, ALU.add, reads=["score"], writes=["score"])
            kb.p.op("dve", lambda en: en.max(top8[:, :], score[:, :]), ["score"], ["top8"])
            kb.reduce("dve", thr[:, :], top8[:, :], ALU.min, reads=["top8"], writes=["thr"])
            kb.ts("dve", negsel[:, :], score[:, :], thr[:, 0:1], NEG, ALU.is_lt, ALU.mult, reads=["score", "thr"], writes=["negsel"])
            for jt in range(2):
                kb.tr(psb[:, 128 + jt * 128:256 + jt * 128], negsel[:, jt * 128:(jt + 1) * 128], identb[:, :],
                      ["negsel", "identb"], ["psb"])
                kb.copy("dve", negselT[:, jt, qb * 128:(qb + 1) * 128], psb[:, 128 + jt * 128:256 + jt * 128],
                        reads=["psb"], writes=["negselT"])
        for mi, h in enumerate(myl):
            items = []
            for kt in range(4 * T + 4):
                mask = masks[:, (kt - 4 * T) * 512:(kt - 4 * T + 1) * 512] if kt >= 4 * T else None
                extra = (EB[:, (kt % 64) * 128:(kt % 64 + 1) * 128], negselT[:, kt // 64, :], ["EB", "negselT"])
                items.append((kt, posb[:, mi * NK + kt:mi * NK + kt + 1], mask, extra))
            attn(T, items, Ska, lambda kt: [("Ska", kt // 4), "Ska1"], Qa[h], ("Qa", h), Sva,
                 lambda kt: [("Sva", kt), "Sva1"], osel[mi, :, ts_])
            items = []
            for m in range(8):
                kt = 4 * T - 4 + m
                if kt < 0:
                    continue
                items.append((kt, posb[:, mi * NK + kt:mi * NK + kt + 1], masks[:, (8 + m) * 512:(9 + m) * 512], None))
            attn(T, items, Wka, lambda kt: [("Wka", kt // 4), "Wka1"], Qa[h], ("Qa", h), Wva,
                 lambda kt: [("Wva", kt), "Wva1"], ow[mi, :, ts_], ring=8)

    IMPx = kb.sb("IMPx", [128, 4, 264])
    kb.memset("pool", IMPx[:, :, :], 0.0, writes=["IMPx"])
    impacc = kb.sb("impacc", [128, 4, 256])
    stream_project(kb, xT, S, 512, wbf, "wbf", consume, nbuf=1, NST=2)
    return kb.finish()


ALPHA = float((2 * 2) ** 0.25)
EPS = 1e-5


def tail_consts():
    identf = np.eye(128, dtype=np.float32)
    selE = np.zeros((16, 16 * 128), np.float32)
    for e in range(16):
        selE[e, e * 128:(e + 1) * 128] = 1.0
    return dict(identf=identf, identb=_bf(identf), selE=selE)


def build_tail(NTOK):
    kb = KB(7)
    nc = kb.nc
    psb = nc.alloc_psum_tensor("psb", [128, 1024], BF16)
    GB = min(2, NTOK // 128)
    GT = GB * 128
    NG = NTOK // GT
    hres_d = kb.din("hres", [NTOK, 2048])
    onsa_d = kb.din("onsa", [NTOK, 16 * 3 * 65])
    gat_d = kb.din("gat", [NTOK, 48])
    osb_d = kb.din("osb", [NTOK, 512])
    ofox_d = kb.din("ofox", [NTOK, 8 * 65])
    prm_d = kb.din("prm", [5, 128, 2048])
    wout_d = kb.din("wout", [2048, 2048])
    rw_d = kb.din("rw", [2048, 20]); rb_d = kb.din("rb", [128, 20])
    wg_d = kb.din("wg", [16, 2048, 256]); wu_d = kb.din("wu", [16, 2048, 256]); wd_d = kb.din("wd", [16, 256, 2048])
    identf_d = kb.din("identf", [128, 128]); identb_d = kb.din("identb", [128, 128], BF16)
    selE_d = kb.din("selE", [16, 2048])
    hout = kb.dout("hout", [NTOK, 2048])

    def ld(name, d, shape, dt=F32):
        t_ = kb.sb(name, shape, dt)
        kb.dma(t_[:, :], d[:, :], "c_" + name, writes=[name])
        return t_
    identf = ld("identf", identf_d, [128, 128]); identb = ld("identb", identb_d, [128, 128], BF16)
    selE = ld("selE", selE_d, [16, 2048]); rb = ld("rb", rb_d, [128, 20])
    rwf = kb.sb("rwf", [128, NCH, 20])
    kb.dma(rwf[:, :, :], rw_d.rearrange("(c p) n -> p c n", p=128), "c_rw", writes=["rwf"])

    prm = [kb.sb(f"prm{i}", [128, 2048]) for i in range(2)]
    pc = {"n": 0}

    def getprm(idx):
        s = pc["n"] % 2
        pc["n"] += 1
        kb.dma(prm[s][:, :], prm_d[idx, :, :], f"prm{s}", writes=[("prm", s)])
        return prm[s], ("prm", s)

    hres = kb.sb("hres", [128, GB, 2048]); ysb = kb.sb("ysb", [128, GB, 2048])
    onsa = kb.sb("onsa", [128, 16 * 3 * 65]); gat = kb.sb("gat", [128, 48]); ofox = kb.sb("ofox", [128, 8 * 65])
    y = kb.sb("y", [128, 2048]); ybf = kb.sb("ybf", [128, 2048], BF16)
    sq = kb.sb("sq", [128, 2048])
    yT = kb.sb("yT", [128, NCH, GT], BF16)
    h1Tf = kb.sb("h1Tf", [128, NCH, 128]); h1Tb = kb.sb("h1Tb", [128, NCH, GT], BF16)
    hid = kb.sb("hid", [128, 32, GT], BF16)
    wstg = [kb.sb(f"wstg{i}", [128, 2048]) for i in range(2)]
    wob = kb.sb("wob", [128, NCH, 512], BF16)
    wgb = kb.sb("wgb", [128, NCH, 256], BF16); wub = kb.sb("wub", [128, NCH, 256], BF16); wdb = kb.sb("wdb", [128, 2, 2048], BF16)
    sm = {k_: kb.sb("sm_" + k_, [128, 64]) for k_ in ("rsum", "eg", "coef", "ss", "lg", "t1", "t2", "gate", "top8")}
    gateT = kb.sb("gateT", [16, GT])
    kb.memset("dve", sm["t1"][:, :], 1.0, writes=["lnr0", "lnm", "lnv"])
    kb.memset("dve", sm["ss"][:, :], 1.0, writes=["ss"])
    t512 = [kb.sb(f"t512_{i}", [128, GT]) for i in range(2)]
    sc = {"w": 0}

    def stage_cast(dst_ap, src_ap, shape_cols, dkey):
        s = sc["w"] % 2
        sc["w"] += 1
        kb.dma(wstg[s][:, 0:shape_cols], src_ap, f"wstg{s}", writes=[("wstg", s)])
        kb.copy(("dve", "pool")[s], dst_ap, wstg[s][:, 0:shape_cols], reads=[("wstg", s)], writes=[dkey])

    def layer_norm(x_ap, xkey, gi, bi_, out_ap, okey):
        mean = sm["t1"][:, 0:1]; var = sm["t1"][:, 1:2]; rstd = sm["t1"][:, 6:7]
        kb.reduce("dve", mean, x_ap, ALU.add, reads=[xkey], writes=["lnm"])
        kb.ts("dve", mean, mean, 1.0 / 2048, None, ALU.mult, reads=["lnm"], writes=["lnm"])
        kb.ts("dve", x_ap, x_ap, mean, None, ALU.subtract, reads=[xkey, "lnm"], writes=[xkey])
        kb.tt("dve", sq[:, :], x_ap, x_ap, ALU.mult, reads=[xkey], writes=["sq"])
        kb.reduce("dve", var, sq[:, :], ALU.add, reads=["sq"], writes=["lnv"])
        kb.ts("dve", sm["t1"][:, 2:3], var, 1.0 / 2048, EPS, ALU.mult, ALU.add, reads=["lnv"], writes=["lnr0"])
        kb.act(sm["t1"][:, 4:6], sm["t1"][:, 2:4], AF.Ln, ["lnr0"], ["lnr1"])
        kb.act(sm["t1"][:, 6:8], sm["t1"][:, 4:6], AF.Exp, ["lnr1"], ["lnr"], scale=-0.5)
        gp, gk = getprm(gi)
        kb.stt("dve", x_ap, x_ap, rstd, gp[:, :], ALU.mult, ALU.mult, reads=[xkey, "lnr", gk], writes=[xkey])
        bp, bk = getprm(bi_)
        kb.tt("dve", out_ap, x_ap, bp[:, :], ALU.add, reads=[xkey, bk], writes=[okey])

    for G in range(NG):
        for bl in range(GB):
            r0 = G * GT + bl * 128
            rows = slice(r0, r0 + 128)
            kb.dma(hres[:, bl, :], hres_d[rows, :], f"hres{bl}", writes=[("hres", bl)])
            kb.dma(onsa[:, :], onsa_d[rows, :], "onsa", writes=["onsa"])
            kb.dma(gat[:, :], gat_d[rows, :], "gat", writes=["gat"])
            kb.dma(ofox[:, :], ofox_d[rows, :], "ofox", writes=["ofox"])
            kb.dma(y[:, 1024:1536], osb_d[rows, :], "osbl", writes=["y"])
            ov = onsa[:, :].rearrange("p (h d) -> p h d", d=65)
            rsum, eg, coef = sm["rsum"][:, 0:48], sm["eg"][:, 0:48], sm["coef"][:, 0:48]
            kb.ts("dve", rsum, ov[:, :, 64], 1e-30, None, ALU.max, reads=["onsa"], writes=["rsum"])
            kb.p.op("dve", lambda en, rsum=rsum: en.reciprocal(rsum, rsum), ["rsum"], ["rsum"])
            kb.act(eg, gat[:, :], AF.Exp, ["gat"], ["eg"], scale=-1.0)
            kb.ts("dve", eg, eg, 1.0, None, ALU.add, reads=["eg"], writes=["eg"])
            kb.p.op("dve", lambda en, eg=eg: en.reciprocal(eg, eg), ["eg"], ["eg"])
            kb.tt("dve", coef, eg, rsum, ALU.mult, reads=["eg", "rsum"], writes=["coef"])
            for h in range(16):
                yo = y[:, h * 64:(h + 1) * 64]
                for b in range(3):
                    hb_ = h * 3 + b
                    if b == 0:
                        kb.ts("dve", yo, ov[:, hb_, 0:64], coef[:, hb_:hb_ + 1], None, ALU.mult, reads=["onsa", "coef"], writes=["y"])
                    else:
                        kb.stt("dve", yo, ov[:, hb_, 0:64], coef[:, hb_:hb_ + 1], yo, ALU.mult, ALU.add,
                               reads=["onsa", "coef", "y"], writes=["y"])
            fv = ofox[:, :].rearrange("p (h d) -> p h d", d=65)
            rf = sm["rsum"][:, 48:56]
            kb.ts("dve", rf, fv[:, :, 64], 1e-30, None, ALU.max, reads=["ofox"], writes=["rf"])
            kb.p.op("dve", lambda en, rf=rf: en.reciprocal(rf, rf), ["rf"], ["rf"])
            for h in range(8):
                kb.ts("dve", y[:, 1536 + h * 64:1536 + (h + 1) * 64], fv[:, h, 0:64], rf[:, h:h + 1], None, ALU.mult,
                      reads=["ofox", "rf"], writes=["y"])
            nw, nk = getprm(0)
            grp = ((0, 1024), (1024, 1536), (1536, 2048))
            for gi_, (a0, a1) in enumerate(grp):
                ss = sm["ss"][:, gi_:gi_ + 1]
                kb.tt("dve", sq[:, a0:a1], y[:, a0:a1], y[:, a0:a1], ALU.mult, reads=["y"], writes=["sq"])
                kb.reduce("dve", ss, sq[:, a0:a1], ALU.add, reads=["sq"], writes=["ss"])
                kb.ts("dve", ss, ss, 1.0 / (a1 - a0), EPS, ALU.mult, ALU.add, reads=["ss"], writes=["ss"])
            kb.act(sm["ss"][:, 4:8], sm["ss"][:, 0:4], AF.Ln, ["ss"], ["ss1"])
            kb.act(sm["ss"][:, 8:12], sm["ss"][:, 4:8], AF.Exp, ["ss1"], ["ss2"], scale=-0.5)
            for gi_, (a0, a1) in enumerate(grp):
                kb.stt("dve", ybf[:, a0:a1], y[:, a0:a1], sm["ss"][:, 8 + gi_:9 + gi_], nw[:, a0:a1], ALU.mult, ALU.mult,
                       reads=["y", "ss2", nk], writes=["ybf"])
            for c0 in range(0, NCH, 4):
                for c in range(c0, c0 + 4):
                    kb.tr(psb[:, (c % 4) * 128:(c % 4 + 1) * 128], ybf[:, c * 128:(c + 1) * 128], identb[:, :], ["ybf", "identb"], ["psb"])
                for c in range(c0, c0 + 4):
                    kb.copy("dve", yT[:, c, bl * 128:(bl + 1) * 128], psb[:, (c % 4) * 128:(c % 4 + 1) * 128],
                            reads=["psb"], writes=["yT"])
        for n in range(4):
            for c4 in range(8):
                src = wout_d[c4 * 256:(c4 + 1) * 256, n * 512:(n + 1) * 512].rearrange("(c p) n -> p c n", p=128)
                s = sc["w"] % 2
                sc["w"] += 1
                kb.dma(wstg[s][:, 0:1024].rearrange("p (c n) -> p c n", n=512), src, f"wstg{s}", writes=[("wstg", s)])
                kb.copy(("dve", "pool")[s], wob[:, 2 * c4:2 * c4 + 2, :], wstg[s][:, 0:1024].rearrange("p (c n) -> p c n", n=512),
                        reads=[("wstg", s)], writes=["wob"])
            for bl in range(GB):
                bk_ = (bl + n * GB) % 2
                b = kb.bank(bk_)
                for c in range(NCH):
                    kb.mm(b, yT[:, c, bl * 128:(bl + 1) * 128], wob[:, c, :], c == 0, c == NCH - 1, ["yT", "wob"], [f"pb{bk_}"])
                kb.stt("dve", hres[:, bl, n * 512:(n + 1) * 512], hres[:, bl, n * 512:(n + 1) * 512], ALPHA, b, ALU.mult, ALU.add,
                       reads=[("hres", bl), f"pb{bk_}"], writes=[("hres", bl)])
        for bl in range(GB):
            layer_norm(hres[:, bl, :], ("hres", bl), 1, 2, hres[:, bl, :], ("hres", bl))
            for c in range(NCH):
                bk_ = 2 + c % 2
                b = kb.bank(bk_)
                kb.tr(b[:, 0:128], hres[:, bl, c * 128:(c + 1) * 128], identf[:, :], [("hres", bl), "identf"], [f"pb{bk_}"])
                kb.copy("dve", h1Tf[:, c, :], b[:, 0:128], reads=[f"pb{bk_}"], writes=["h1Tf"])
                kb.copy("pool", h1Tb[:, c, bl * 128:(bl + 1) * 128], h1Tf[:, c, :], reads=["h1Tf"], writes=["h1Tb"])
            b = kb.bank(4)
            for c in range(NCH):
                kb.mm(b[:, 0:20], h1Tf[:, c, :], rwf[:, c, :], c == 0, c == NCH - 1, ["h1Tf", "rwf"], ["pb4"])
            lg = sm["lg"][:, 0:20]
            kb.tt("dve", lg, b[:, 0:20], rb[:, :], ALU.add, reads=["pb4", "rb"], writes=["lg"])
            mx = sm["t2"][:, 0:1]; gw = sm["t2"][:, 1:2]; v1 = sm["t2"][:, 2:3]; w1_ = sm["t2"][:, 3:4]; w2_ = sm["t2"][:, 4:5]
            nmx = sm["t2"][:, 5:6]; dv = sm["t2"][:, 6:7]
            oh = sm["t2"][:, 8:12]; eg4 = sm["t2"][:, 12:16]; sel16 = sm["t2"][:, 16:32]; m1 = sm["t2"][:, 32:48]; m2 = sm["t2"][:, 48:64]
            kb.reduce("dve", mx, lg[:, 0:4], ALU.max, reads=["lg"], writes=["r_mx"])
            kb.ts("dve", oh, lg[:, 0:4], mx, None, ALU.is_ge, reads=["lg", "r_mx"], writes=["r_oh"])
            kb.ts("dve", nmx, mx, -1.0, None, ALU.mult, reads=["r_mx"], writes=["r_nmx"])
            kb.act(eg4, lg[:, 0:4], AF.Exp, ["lg", "r_nmx"], ["r_eg4"], bias=nmx)
            kb.reduce("dve", gw, eg4, ALU.add, reads=["r_eg4"], writes=["r_gw"])
            kb.p.op("dve", lambda en, gw=gw: en.reciprocal(gw, gw), ["r_gw"], ["r_gw"])
            kb.ts("dve", oh, oh, 1.0, 1e30, ALU.subtract, ALU.mult, reads=["r_oh"], writes=["r_oh"])
            for g_ in range(4):
                kb.ts("dve", sel16[:, g_ * 4:(g_ + 1) * 4], lg[:, 4 + g_ * 4:8 + g_ * 4], oh[:, g_:g_ + 1], None, ALU.add,
                      reads=["lg", "r_oh"], writes=["r_sel"])
            top8 = sm["top8"][:, 0:8]
            sel2 = sm["top8"][:, 16:32]
            kb.reduce("dve", top8[:, 0:1], sel16, ALU.max, reads=["r_sel"], writes=["r_top8"])
            kb.ts("dve", m1, sel16, top8[:, 0:1], None, ALU.is_equal, reads=["r_sel", "r_top8"], writes=["r_m1"])
            kb.stt("dve", sel2, m1, -1e30, sel16, ALU.mult, ALU.add, reads=["r_m1", "r_sel"], writes=["r_sel2"])
            kb.reduce("dve", top8[:, 1:2], sel2, ALU.max, reads=["r_sel2"], writes=["r_top8"])
            kb.ts("dve", m2, sel2, top8[:, 1:2], None, ALU.is_equal, reads=["r_sel2", "r_top8"], writes=["r_m2"])
            kb.tt("dve", dv, top8[:, 1:2], top8[:, 0:1], ALU.subtract, reads=["r_top8"], writes=["r_dv"])
            kb.act(w1_, dv, AF.Exp, ["r_dv"], ["r_w1"])
            kb.ts("dve", w1_, w1_, 1.0, None, ALU.add, reads=["r_w1"], writes=["r_w1"])
            kb.p.op("dve", lambda en, w1_=w1_: en.reciprocal(w1_, w1_), ["r_w1"], ["r_w1"])
            kb.ts("dve", w2_, w1_, -1.0, 1.0, ALU.mult, ALU.add, reads=["r_w1"], writes=["r_w2"])
            kb.tt("dve", w1_, w1_, gw, ALU.mult, reads=["r_w1", "r_gw"], writes=["r_w1"])
            kb.tt("dve", w2_, w2_, gw, ALU.mult, reads=["r_w2", "r_gw"], writes=["r_w2"])
            gate = sm["gate"][:, 0:16]
            kb.ts("dve", gate, m1, w1_, None, ALU.mult, reads=["r_m1", "r_w1"], writes=["gate"])
            kb.stt("dve", gate, m2, w2_, gate, ALU.mult, ALU.add, reads=["r_m2", "r_w2", "gate"], writes=["gate"])
            b = kb.bank(4)
            kb.tr(b[0:16, 128:256], gate, identf[:, :], ["gate", "identf"], ["pb4"])
            kb.copy("dve", gateT[:, bl * 128:(bl + 1) * 128], b[0:16, 128:256], reads=["pb4"], writes=["gateT"])
        for e in range(16):
            for (dst, src_d, key) in ((wgb, wg_d, "wgb"), (wub, wu_d, "wub")):
                for half in range(2):
                    s = sc["w"] % 2
                    sc["w"] += 1
                    kb.dma(wstg[s][:, :].rearrange("p (c n) -> p c n", n=256),
                           src_d[e, half * 1024:(half + 1) * 1024, :].rearrange("(c p) n -> p c n", p=128), f"wstg{s}", writes=[("wstg", s)])
                    kb.copy(("dve", "pool")[s], dst[:, half * 8:(half + 1) * 8, :], wstg[s][:, :].rearrange("p (c n) -> p c n", n=256),
                            reads=[("wstg", s)], writes=[key])
            for fc in range(2):
                s = sc["w"] % 2
                sc["w"] += 1
                kb.dma(wstg[s][:, :], wd_d[e, fc * 128:(fc + 1) * 128, :], f"wstg{s}", writes=[("wstg", s)])
                kb.copy(("dve", "pool")[s], wdb[:, fc, :], wstg[s][:, :], reads=[("wstg", s)], writes=["wdb"])
            for fc in range(2):
                bg, bu, bgt = kb.bank(2), kb.bank(3), kb.bank(4)
                for c in range(NCH):
                    kb.mm(bg[:, 0:GT], wgb[:, c, fc * 128:(fc + 1) * 128], h1Tb[:, c, :], c == 0, c == NCH - 1, ["wgb", "h1Tb"], ["pb2"])
                for c in range(NCH):
                    kb.mm(bu[:, 0:GT], wub[:, c, fc * 128:(fc + 1) * 128], h1Tb[:, c, :], c == 0, c == NCH - 1, ["wub", "h1Tb"], ["pb3"])
                kb.mm(bgt[:, 0:GT], selE[0:16, e * 128:(e + 1) * 128], gateT[0:16, :], True, True, ["selE", "gateT"], ["pb4"])
                ta, tb = t512[0], t512[1]
                kb.act(ta[:, :], bg[:, 0:GT], AF.Silu, ["pb2"], ["t512a"])
                kb.tt("dve", tb[:, :], ta[:, :], bu[:, 0:GT], ALU.mult, reads=["t512a", "pb3"], writes=["t512b"])
                kb.tt("dve", hid[:, e * 2 + fc, :], tb[:, :], bgt[:, 0:GT], ALU.mult, reads=["t512b", "pb4"], writes=[("hid", e)])
            for bl in range(GB):
                for n in range(4):
                    bk_ = 5 + (bl * 4 + n) % 2
                    b = kb.bank(bk_)
                    for fc in range(2):
                        kb.mm(b, hid[:, e * 2 + fc, bl * 128:(bl + 1) * 128], wdb[:, fc, n * 512:(n + 1) * 512], fc == 0, fc == 1,
                              [("hid", e), "wdb"], [f"pb{bk_}"])
                    ydst = ysb[:, bl, n * 512:(n + 1) * 512]
                    if e == 0:
                        kb.copy("dve", ydst, b, reads=[f"pb{bk_}"], writes=[("ysb", bl)])
                    else:
                        kb.tt("dve", ydst, ydst, b, ALU.add, reads=[f"pb{bk_}", ("ysb", bl)], writes=[("ysb", bl)])
        for bl in range(GB):
            r0 = G * GT + bl * 128
            kb.stt("dve", ysb[:, bl, :], hres[:, bl, :], ALPHA, ysb[:, bl, :], ALU.mult, ALU.add,
                   reads=[("hres", bl), ("ysb", bl)], writes=[("ysb", bl)])
            layer_norm(ysb[:, bl, :], ("ysb", bl), 3, 4, ysb[:, bl, :], ("ysb", bl))
            kb.dma(hout[r0:r0 + 128, :], ysb[:, bl, :], f"out{bl}", reads=[("ysb", bl)])
    return kb.finish()


_CACHE = {}
OFF = np.cumsum([0, 1024, 256, 256, 256, 256, 256, 256, 48, 512, 512, 512, 512, 512, 512, 8])


def _prog(name, fn):
    if name not in _CACHE:
        _CACHE[name] = fn()
    return _CACHE[name]


def _run(nc, in_maps):
    res = run_bass_kernel_spmd(nc, in_maps, core_ids=list(range(len(in_maps))))
    return res.results


def _layer(h, l, P, S):
    C = 8
    w_in = P["w_in"][l]
    col = lambda i: w_in[:, OFF[i]:OFF[i + 1]]
    hT = np.ascontiguousarray(h.T)
    fq, fk, fv, ff = col(11), col(12), col(13), col(14)
    cf, cb = fox_consts()
    ims = []
    for c in range(C):
        hs = slice(64 * c, 64 * c + 64)
        w = np.ascontiguousarray(np.concatenate([fq[:, hs], fk[:, hs], fv[:, hs], ff[:, c:c + 1]], axis=1))
        fb = np.ascontiguousarray(np.broadcast_to(P["fox_forget_bias"][l, c], (128, 1))).astype(np.float32)
        ims.append({"xT": hT, "w": w, "fb": fb, "cf": cf, "cb": cb})
    r_fox = _run(_prog(("fox", S), lambda: build_fox(S)), ims)
    sq_, sk_, sv_ = col(8), col(9), col(10)
    cbs = sb_consts()
    ims = []
    for c in range(C):
        hs = slice(64 * c, 64 * c + 64)
        w = np.ascontiguousarray(np.concatenate([sq_[:, hs], sk_[:, hs], sv_[:, hs]], axis=1))
        ims.append({"xT": hT, "w": w, "cb": cbs})
    r_sb = _run(_prog(("sb", S), lambda: build_sb(S)), ims)
    nq, ck, cv, sk, sv, wk, wv, ng = (col(i) for i in range(8))
    w1 = np.ascontiguousarray(np.concatenate([P["cmp_w1_k"][l].transpose(1, 0, 2).reshape(64, -1),
                                              P["cmp_w1_v"][l].transpose(1, 0, 2).reshape(64, -1)], axis=1))
    w2 = np.ascontiguousarray(np.concatenate([P["cmp_w2_k"][l], P["cmp_w2_v"][l]], axis=1))
    pT = np.ascontiguousarray(np.concatenate([P["cmp_pos_k"][l].T, P["cmp_pos_v"][l].T], axis=1))
    ims = []
    for c in range(C):
        g, hh = c // 2, c % 2
        my = [4 * g + 2 * hh, 4 * g + 2 * hh + 1]
        hord = my + [4 * g + 2 * (1 - hh), 4 * g + 2 * (1 - hh) + 1]
        gs = slice(64 * g, 64 * g + 64)
        w = np.concatenate([nq[:, 64 * h_:64 * h_ + 64] for h_ in hord] +
                           [ck[:, gs], cv[:, gs], sk[:, gs], wk[:, gs], sv[:, gs], wv[:, gs], ng[:, 3 * my[0]:3 * my[0] + 6]], axis=1)
        im = {"xT": hT, "w": np.ascontiguousarray(w), "w1": w1, "w2": w2, "pT": pT}
        im.update(nsa_consts(S, c))
        ims.append(im)
    r_nsa = _run(_prog(("nsa", S), lambda: build_nsa(S)), ims)
    onsa = np.empty((S, 16, 3, 65), np.float32)
    gat = np.empty((S, 16, 3), np.float32)
    for c in range(C):
        for mi in range(2):
            hd = 2 * c + mi
            for bi, nm in enumerate(("oc", "os", "ow")):
                onsa[:, hd, bi, :] = r_nsa[c][nm][mi].T
            gat[:, hd, :] = r_nsa[c]["gT"][3 * mi:3 * mi + 3].T
    osb = np.concatenate([r_sb[c]["oT"].T for c in range(C)], axis=1)
    ofox = np.concatenate([r_fox[c]["oT"].T for c in range(C)], axis=1)
    onsa = onsa.reshape(S, -1); gat = gat.reshape(S, -1)
    NTOK = S // C
    normw = np.concatenate([P["norm_nsa"][l], P["norm_sb"][l], P["norm_fox"][l]])
    prm = np.stack([np.broadcast_to(v, (128, 2048)) for v in
                    (normw, P["ln1_g"][l], P["ln1_b"][l], P["ln2_g"][l], P["ln2_b"][l])]).astype(np.float32)
    prm = np.ascontiguousarray(prm)
    rw = np.ascontiguousarray(np.concatenate([P["router_group_w"][l], P["router_expert_w"][l]], axis=1))
    rb = np.ascontiguousarray(np.broadcast_to(np.concatenate([P["router_group_b"][l], P["router_expert_b"][l]]), (128, 20))).astype(np.float32)
    tc_ = tail_consts()
    ims = []
    for c in range(C):
        rs_ = slice(c * NTOK, (c + 1) * NTOK)
        im = {"hres": np.ascontiguousarray(h[rs_]), "onsa": np.ascontiguousarray(onsa[rs_]), "gat": np.ascontiguousarray(gat[rs_]),
              "osb": np.ascontiguousarray(osb[rs_]), "ofox": np.ascontiguousarray(ofox[rs_]), "prm": prm,
              "wout": P["w_out"][l], "rw": rw, "rb": rb, "wg": P["expert_w_gate"][l], "wu": P["expert_w_up"][l],
              "wd": P["expert_w_down"][l]}
        im.update(tc_)
        ims.append(im)
    r_t = _run(_prog(("tail", NTOK), lambda: build_tail(NTOK)), ims)
    return np.concatenate([r_t[c]["hout"] for c in range(C)], axis=0)


def kernel(**inputs):
    P = {k: np.asarray(v) for k, v in inputs.items()}
    x = P["x"]
    S = x.shape[1]
    h = np.ascontiguousarray(x[0])
    for l in range(2):
        h = _layer(h, l, P, S)
    return h[None].astype(np.float32)
```

```python
import contextlib
import numpy as np
import concourse.bass as bass
import concourse.mybir as mybir

F32 = mybir.dt.float32
BF16 = mybir.dt.bfloat16
AF = mybir.ActivationFunctionType
ALU = mybir.AluOpType
AX = mybir.AxisListType

CENGS = ("pe", "act", "dve", "pool", "sp")


class Prog:
    def __init__(self, nc):
        self.nc = nc
        self.ops = {e: [] for e in CENGS}
        self.cnt = {}
        self.waited = {e: {} for e in CENGS}
        self.lastw = {}
        self.readers = {}
        self.sems = {}
        self.nops = 0

    def _issue(self, eng, chan, fn, reads, writes, accum):
        deps = {}
        def add(d):
            if d is not None and deps.get(d[0], 0) < d[1]:
                deps[d[0]] = d[1]
        for k in reads:
            add(self.lastw.get(k))
        for k in writes:
            lw = self.lastw.get(k)
            if not (accum and lw is not None and lw[0] == "pe" and eng == "pe"):
                add(lw)
            for x, n in self.readers.get(k, {}).items():
                add((x, n))
        waits = []
        wd = self.waited[eng]
        for x, n in deps.items():
            if wd.get(x, 0) < n:
                wd[x] = n
                waits.append((x, n))
        self.cnt[chan] = self.cnt.get(chan, 0) + 1
        idx = self.cnt[chan]
        self.ops[eng].append((waits, fn, chan))
        for k in reads:
            self.readers.setdefault(k, {})[chan] = idx
        for k in writes:
            self.lastw[k] = (chan, idx)
            self.readers[k] = {}
        self.nops += 1
        return idx

    def op(self, eng, fn, reads=(), writes=(), accum=False):
        return self._issue(eng, eng, fn, reads, writes, accum)

    def dma(self, fn, slot, reads=(), writes=(), eng="sp"):
        return self._issue(eng, "dma:" + str(slot), fn, reads, writes, False)

    def coll(self, fn, reads=(), writes=()):
        self.ncoll = getattr(self, "ncoll", 0) + 1
        return self._issue("pool", "cc:%d" % self.ncoll, fn, reads, writes, False)

    def final_wait(self, eng="sp"):
        waits = [(c, n) for c, n in self.cnt.items()]
        self.ops[eng].append((waits, None, None))

    def emit(self):
        nc = self.nc
        with contextlib.ExitStack() as st:
            for c in self.cnt:
                self.sems[c] = st.enter_context(nc.semaphore("s_" + c.replace(":", "_")))
            block = st.enter_context(nc.Block())
            dec = {"pe": block.tensor, "act": block.scalar, "dve": block.vector,
                   "pool": block.gpsimd, "sp": block.sync}
            for e in CENGS:
                ops = self.ops[e]
                if not ops:
                    continue
                def body(eng, e=e, ops=ops):
                    for waits, fn, chan in ops:
                        for (x, m) in waits:
                            eng.wait_ge(self.sems[x], m * (16 if x.startswith("dma:") else 1))
                        if fn is None:
                            continue
                        ins = fn(eng)
                        if chan.startswith("cc:"):
                            ins.then_inc(self.sems[chan])
                        else:
                            ins.then_inc(self.sems[chan], 16 if chan.startswith("dma:") else 1)
                dec[e](body)

from concourse.bass_utils import run_bass_kernel_spmd
import ml_dtypes

NEG = -30000.0
D_MODEL = 2048
NCH = 16


def _bf(a):
    return np.asarray(a, np.float32).astype(ml_dtypes.bfloat16)


class KB:
    def __init__(self, nb=8):
        self.nc = bass.Bass("TRN2", target_bir_lowering=False)
        self.p = Prog(self.nc)
        self.psum = self.nc.alloc_psum_tensor("psum", [128, nb * 512], F32)
        self._cast_rr = 0

    def din(self, name, shape, dt=F32):
        return self.nc.dram_tensor(name, list(shape), dt, kind="ExternalInput").ap()

    def dout(self, name, shape, dt=F32):
        return self.nc.dram_tensor(name, list(shape), dt, kind="ExternalOutput").ap()

    def dscr(self, name, shape, dt=F32):
        return self.nc.dram_tensor(name, list(shape), dt, kind="Internal").ap()

    def sb(self, name, shape, dt=F32):
        return self.nc.alloc_sbuf_tensor("sb_" + name, list(shape), dt)

    def bank(self, b):
        return self.psum[:, b * 512:(b + 1) * 512]

    def dma(self, out, in_, slot, reads=(), writes=(), eng="sp"):
        self.p.dma(lambda e: e.dma_start(out=out, in_=in_), slot, reads, writes, eng)

    def mm(self, out, lhsT, rhs, start, stop, reads, writes, sgc=False):
        self.p.op("pe", lambda e: e.matmul(out, lhsT, rhs, start=start, stop=stop, skip_group_check=sgc),
                  reads, writes, accum=not start)

    def tr(self, out, in_, ident, reads, writes):
        self.p.op("pe", lambda e: e.transpose(out, in_, ident), reads, writes)

    def act(self, out, in_, func, reads, writes, bias=0.0, scale=1.0):
        self.p.op("act", lambda e: e.activation(out, in_, func, bias=bias, scale=scale), reads, writes)

    def ts(self, eng, out, in0, s1, s2, op0, op1=None, reads=(), writes=()):
        if op1 is None:
            self.p.op(eng, lambda e: e.tensor_scalar(out, in0, s1, s2, op0), reads, writes)
        else:
            self.p.op(eng, lambda e: e.tensor_scalar(out, in0, s1, s2, op0, op1), reads, writes)

    def tt(self, eng, out, in0, in1, op, reads=(), writes=()):
        self.p.op(eng, lambda e: e.tensor_tensor(out, in0, in1, op), reads, writes)

    def stt(self, eng, out, in0, scalar, in1, op0, op1, reads=(), writes=()):
        self.p.op(eng, lambda e: e.scalar_tensor_tensor(out, in0, scalar, in1, op0, op1), reads, writes)

    def copy(self, eng, out, in_, reads=(), writes=()):
        if eng == "act":
            self.p.op(eng, lambda e: e.copy(out, in_), reads, writes)
        else:
            self.p.op(eng, lambda e: e.tensor_copy(out, in_), reads, writes)

    def memset(self, eng, ap, val, writes=()):
        self.p.op(eng, lambda e: e.memset(ap, val), (), writes)

    def reduce(self, eng, out, in_, op, reads=(), writes=()):
        self.p.op(eng, lambda e: e.tensor_reduce(out, in_, AX.X, op), reads, writes)

    def finish(self):
        self.p.final_wait("sp")
        self.p.emit()
        return self.nc


def stream_project(kb, xT, S, TT, wbf, wkey, consume, nbuf=2, NST=4):
    NT = S // TT
    stg = [kb.sb(f"xstg{i}", [128, TT], F32) for i in range(NST)]
    hT = [kb.sb(f"hT{i}", [128, NCH, TT], BF16) for i in range(nbuf)]
    n = 0
    for T in range(NT):
        hb = hT[T % nbuf]
        hkey = ("hT", T % nbuf)
        for c in range(NCH):
            s = n % NST
            kb.dma(stg[s][:, :], xT[c * 128:(c + 1) * 128, T * TT:(T + 1) * TT], f"xs{s}",
                   writes=[("xstg", s)])
            eng = ("dve", "pool")[n % 2]
            kb.copy(eng, hb[:, c, :], stg[s][:, :], reads=[("xstg", s)], writes=[(hkey, c)])
            n += 1
        consume(T, hb, [(hkey, c) for c in range(NCH)])


def fox_consts():
    i = np.arange(128)
    ident = np.eye(128, dtype=np.float32)
    tri_inc = (i[:, None] <= i[None, :]).astype(np.float32)
    ones = np.ones((128, 128), np.float32)
    U = (i[:, None] < i[None, :]).astype(np.float32)
    cf = np.concatenate([ident, tri_inc, ones, U], axis=1)
    q = np.arange(512)
    masks = [np.where(128 * j + i[:, None] <= q[None, :], 0.0, NEG) for j in range(4)]
    cb = _bf(np.concatenate(masks, axis=1))
    return cf, cb


def build_fox(S):
    kb = KB()
    nc = kb.nc
    NT, NK = S // 512, S // 128
    xT = kb.din("xT", [D_MODEL, S])
    w = kb.din("w", [D_MODEL, 193])
    fb = kb.din("fb", [128, 1])
    cf_d = kb.din("cf", [128, 512])
    cb_d = kb.din("cb", [128, 2048], BF16)
    oT = kb.dout("oT", [65, S])
    scr = kb.dscr("scr", [1, S], BF16)

    cf = kb.sb("cf", [128, 512]); cb = kb.sb("cb", [128, 2048], BF16)
    kb.dma(cf[:, :], cf_d[:, :], "c0", writes=["cf"])
    kb.dma(cb[:, :], cb_d[:, :], "c1", writes=["cb"])
    ident, tri_inc, ones, U = (cf[:, 0:128], cf[:, 128:256], cf[:, 256:384], cf[:, 384:512])
    fbs = kb.sb("fbs", [128, 1]); nfb = kb.sb("nfb", [128, 1])
    kb.dma(fbs[:, :], fb[:, :], "c2", writes=["fbs"])
    kb.ts("dve", nfb[:, :], fbs[:, :], -1.0, None, ALU.mult, reads=["fbs"], writes=["nfb"])

    wf = kb.sb("wf", [128, NCH, 193]); wbf = kb.sb("wbf", [128, NCH, 196], BF16)
    kb.dma(wf[:, :, :], w.rearrange("(c p) n -> p c n", p=128), "c3", writes=["wf"])
    kb.memset("dve", wbf[:, :, :], 0.0, writes=["wbf"])
    kb.copy("dve", wbf[:, :, 0:193], wf[:, :, :], reads=["wf"], writes=["wbf"])

    Qa = kb.sb("Qa", [65, S], BF16); Ka = kb.sb("Ka", [65, S], BF16)
    Va = kb.sb("Va", [128, NK, 65], BF16)
    FF2 = kb.sb("FF2", [128, NK])
    kb.memset("pool", Ka[64:65, :], 1.0, writes=["Ka1"])
    kb.memset("pool", Va[:, :, 64:65], 1.0, writes=["Va1"])

    def consume(T, hb, hkeys):
        ts_ = slice(T * 512, (T + 1) * 512)
        import os
        CS = int(os.environ.get("CS", "9"))
        if CS == 0:
            return
        b = kb.bank(0)
        for c in range(NCH):
            kb.mm(b[0:64, :], wbf[:, c, 0:64], hb[:, c, :], c == 0, c == NCH - 1, ["wbf", hkeys[c]], ["pb0"])
        kb.ts("dve", Qa[0:64, ts_], b[0:64, :], 0.125, None, ALU.mult, reads=["pb0"], writes=[("Qa", T)])
        if CS == 1:
            return
        b = kb.bank(1)
        for c in range(NCH):
            kb.mm(b[0:64, :], wbf[:, c, 64:128], hb[:, c, :], c == 0, c == NCH - 1, ["wbf", hkeys[c]], ["pb1"])
        kb.copy("act", Ka[0:64, ts_], b[0:64, :], reads=["pb1"], writes=[("Ka", T)])
        if CS == 2:
            return
        for j in range(4):
            kt = 4 * T + j
            bi = j % 2
            b = kb.bank(bi)
            for c in range(NCH):
                kb.mm(b[:, 0:66], hb[:, c, j * 128:(j + 1) * 128], wbf[:, c, 128:194], c == 0, c == NCH - 1,
                      ["wbf", hkeys[c]], [f"pb{bi}"])
            if CS == 3:
                continue
            kb.copy("dve", Va[:, kt, 0:64], b[:, 0:64], reads=[f"pb{bi}"], writes=[("Va", kt)])
            if CS == 4:
                continue
            kb.copy("dve", FF2[:, kt:kt + 1], b[:, 64:65], reads=[f"pb{bi}"], writes=["FF2"])

    import os
    STOP = int(os.environ.get("STOP", "99"))
    if STOP == 0:
        return kb.finish()
    stream_project(kb, xT, S, 512, wbf, "wbf", consume)
    if STOP == 1:
        return kb.finish()

    E = kb.sb("E", [128, NK]); SP2 = kb.sb("SP2", [128, NK])
    kb.act(E[:, :], FF2[:, :], AF.Exp, ["FF2", "nfb"], ["E"], bias=nfb[:, 0:1], scale=-1.0)
    kb.act(SP2[:, :], E[:, :], AF.Ln, ["E"], ["SP2"], bias=1.0, scale=1.0)
    b0 = kb.bank(0)
    kb.tr(b0[0:NK, 0:128], SP2[:, :], ident, ["SP2", "cf"], ["pb0"])
    tot = kb.sb("tot", [128, 1]); TotB = kb.sb("TotB", [128, 128])
    kb.reduce("dve", tot[0:NK, :], b0[0:NK, 0:128], ALU.add, reads=["pb0"], writes=["tot"])
    kb.ts("dve", TotB[0:NK, :], ones[0:NK, :], tot[0:NK, 0:1], None, ALU.mult, reads=["cf", "tot"], writes=["TotB"])
    b1 = kb.bank(1)
    kb.mm(b1[:, 0:NK], tri_inc, SP2[:, :], True, False, ["cf", "SP2"], ["pb1"])
    kb.mm(b1[:, 0:NK], TotB[0:NK, :], U[0:NK, 0:NK], False, True, ["TotB", "cf"], ["pb1"])
    D2 = kb.sb("D2", [128, NK])
    kb.copy("dve", D2[:, :], b1[:, 0:NK], reads=["pb1"], writes=["D2"])
    kb.tr(b0[0:NK, 0:128], D2[:, :], ident, ["D2", "cf"], ["pb0"])
    Drow = kb.sb("Drow", [128, 128], BF16)
    kb.ts("dve", Drow[0:NK, :], b0[0:NK, 0:128], -1.0, None, ALU.mult, reads=["pb0"], writes=["Drow"])
    kb.dma(scr.rearrange("o (k p) -> (o k) p", p=128), Drow[0:NK, :], "c4", reads=["Drow"], writes=["scr"])
    kb.dma(Qa[64:65, :], scr[:, :], "c5", reads=["scr"], writes=["Qa1"])

    if STOP == 2:
        return kb.finish()
    NP = 3
    Pb = [kb.sb(f"P{i}", [128, 512], BF16) for i in range(NP)]
    tmp = [kb.sb(f"tmp{i}", [128, 512]) for i in range(2)]
    osb = [kb.sb(f"osb{i}", [65, 512]) for i in range(2)]
    pairs = [(T, kt) for T in range(NT) for kt in range(4 * T + 4)]
    allQ = [("Qa", T) for T in range(NT)]

    def stage1(i):
        T, kt = pairs[i]
        sb_i = 2 + i % 4
        ps = kb.bank(sb_i)
        kb.mm(ps, Ka[0:65, kt * 128:(kt + 1) * 128], Qa[0:65, T * 512:(T + 1) * 512], True, True,
              [("Ka", kt // 4), "Ka1", ("Qa", T), "Qa1"], [f"pb{sb_i}"])
        pk = ("P", i % NP)
        if kt >= 4 * T:
            j = kt - 4 * T
            tk = ("tmp", i % 2)
            kb.tt("dve", tmp[i % 2][:, :], ps, cb[:, j * 512:(j + 1) * 512], ALU.add, reads=[f"pb{sb_i}", "cb"], writes=[tk])
            kb.act(Pb[i % NP][:, :], tmp[i % 2][:, :], AF.Exp, [tk, "D2"], [pk], bias=D2[:, kt:kt + 1])
        else:
            kb.act(Pb[i % NP][:, :], ps, AF.Exp, [f"pb{sb_i}", "D2"], [pk], bias=D2[:, kt:kt + 1])

    def stage2(i):
        T, kt = pairs[i]
        ob = 6 + T % 2
        po = kb.bank(ob)
        last = kt == 4 * T + 3
        kb.mm(po[0:65, :], Va[:, kt, 0:65], Pb[i % NP][:, :], kt == 0, last,
              [("Va", kt), "Va1", ("P", i % NP)], [f"pb{ob}"])
        if last:
            kb.copy("dve", osb[T % 2][:, :], po[0:65, :], reads=[f"pb{ob}"], writes=[("osb", T % 2)])
            kb.dma(oT[:, T * 512:(T + 1) * 512], osb[T % 2][:, :], f"o{T % 2}", reads=[("osb", T % 2)])

    n = len(pairs)
    LA = 2
    for i in range(n + LA):
        if i < n:
            stage1(i)
        if i >= LA:
            stage2(i - LA)
    return kb.finish()


def sb_consts():
    i = np.arange(128)
    tri = (i[:, None] >= i[None, :]).astype(np.float32)
    ntri = 1.0 - tri
    q = np.arange(512)
    masks = [np.where(128 * j + i[:, None] < q[None, :], 1.0, 0.0) for j in range(4)]
    cb = _bf(np.concatenate([tri, ntri] + masks, axis=1))
    return cb


def build_sb(S):
    kb = KB()
    NT, NK = S // 512, S // 128
    xT = kb.din("xT", [D_MODEL, S])
    w = kb.din("w", [D_MODEL, 192])
    cb_d = kb.din("cb", [128, 2304], BF16)
    oT = kb.dout("oT", [64, S])
    cb = kb.sb("cb", [128, 2304], BF16)
    kb.dma(cb[:, :], cb_d[:, :], "c1", writes=["cb"])
    tri, ntri = cb[:, 0:128], cb[:, 128:256]
    wf = kb.sb("wf", [128, NCH, 192]); wbf = kb.sb("wbf", [128, NCH, 192], BF16)
    kb.dma(wf[:, :, :], w.rearrange("(c p) n -> p c n", p=128), "c3", writes=["wf"])
    kb.copy("dve", wbf[:, :, :], wf[:, :, :], reads=["wf"], writes=["wbf"])
    Qa = kb.sb("Qa", [64, S], BF16); Ka = kb.sb("Ka", [64, S], BF16)
    Va = kb.sb("Va", [128, NK, 64], BF16)

    def consume(T, hb, hkeys):
        ts_ = slice(T * 512, (T + 1) * 512)
        b = kb.bank(0)
        for c in range(NCH):
            kb.mm(b[0:64, :], wbf[:, c, 0:64], hb[:, c, :], c == 0, c == NCH - 1, ["wbf", hkeys[c]], ["pb0"])
        kb.ts("dve", Qa[0:64, ts_], b[0:64, :], 0.125, None, ALU.mult, reads=["pb0"], writes=[("Qa", T)])
        b = kb.bank(1)
        for c in range(NCH):
            kb.mm(b[0:64, :], wbf[:, c, 64:128], hb[:, c, :], c == 0, c == NCH - 1, ["wbf", hkeys[c]], ["pb1"])
        kb.copy("dve", Ka[0:64, ts_], b[0:64, :], reads=["pb1"], writes=[("Ka", T)])
        for j in range(4):
            kt = 4 * T + j
            bi = j % 2
            b = kb.bank(bi)
            for c in range(NCH):
                kb.mm(b[:, 0:64], hb[:, c, j * 128:(j + 1) * 128], wbf[:, c, 128:192], c == 0, c == NCH - 1,
                      ["wbf", hkeys[c]], [f"pb{bi}"])
            kb.copy("dve", Va[:, kt, 0:64], b[:, 0:64], reads=[f"pb{bi}"], writes=[("Va", kt)])

    stream_project(kb, xT, S, 512, wbf, "wbf", consume)

    NB = 3
    E1 = [kb.sb(f"E1_{i}", [128, 512]) for i in range(NB)]
    SPb = [kb.sb(f"SP_{i}", [128, 512], BF16) for i in range(NB)]
    E2 = [kb.sb(f"E2_{i}", [128, 512]) for i in range(NB)]
    Ab = [kb.sb(f"A_{i}", [128, 512], BF16) for i in range(NB)]
    osb = [kb.sb(f"osb{i}", [64, 512]) for i in range(2)]
    pairs = [(T, kt) for T in range(NT) for kt in range(4 * T + 3, -1, -1)]
    n = len(pairs)

    def st1(i):
        T, kt = pairs[i]
        zb = 2 + i % 2
        ps = kb.bank(zb)
        s = i % NB
        kb.mm(ps, Ka[0:64, kt * 128:(kt + 1) * 128], Qa[0:64, T * 512:(T + 1) * 512], True, True,
              [("Ka", kt // 4), ("Qa", T)], [f"pb{zb}"])
        kb.act(E1[s][:, :], ps, AF.Exp, [f"pb{zb}"], [("E1", s)])
        if kt >= 4 * T:
            j = kt - 4 * T
            kb.tt("pool", E1[s][:, :], E1[s][:, :], cb[:, 256 + j * 512:256 + (j + 1) * 512], ALU.mult,
                  reads=[("E1", s), "cb"], writes=[("E1", s)])
        kb.act(SPb[s][:, :], E1[s][:, :], AF.Ln, [("E1", s)], [("SP", s)], bias=1.0)

    def st2(i):
        T, kt = pairs[i]
        lb = 4 + T % 2
        pl = kb.bank(lb)
        s = i % NB
        first = kt == 4 * T + 3
        kb.mm(pl, tri, SPb[s][:, :], first, True, ["cb", ("SP", s)], [f"pb{lb}"], sgc=True)
        kb.act(E2[s][:, :], pl, AF.Exp, [f"pb{lb}"], [("E2", s)], scale=-1.0)
        kb.tt("dve", Ab[s][:, :], E1[s][:, :], E2[s][:, :], ALU.mult, reads=[("E1", s), ("E2", s)], writes=[("A", s)])
        if kt > 0:
            kb.mm(pl, ntri, SPb[s][:, :], False, True, ["cb", ("SP", s)], [f"pb{lb}"], sgc=True)

    def st3(i):
        T, kt = pairs[i]
        ob = 6 + T % 2
        po = kb.bank(ob)
        s = i % NB
        first = kt == 4 * T + 3
        last = kt == 0
        kb.mm(po[0:64, :], Va[:, kt, 0:64], Ab[s][:, :], first, last, [("Va", kt), ("A", s)], [f"pb{ob}"])
        if last:
            kb.copy("dve", osb[T % 2][:, :], po[0:64, :], reads=[f"pb{ob}"], writes=[("osb", T % 2)])
            kb.dma(oT[:, T * 512:(T + 1) * 512], osb[T % 2][:, :], f"o{T % 2}", reads=[("osb", T % 2)])

    for i in range(n + 2):
        if i < n:
            st1(i)
        if 1 <= i <= n:
            st2(i - 1)
        if i >= 2:
            st3(i - 2)
    return kb.finish()


def nsa_slopes():
    return (2.0 ** (-8.0 * np.arange(1, 17, dtype=np.float32) / 16)).astype(np.float32)


def nsa_consts(S, core):
    g, hh = core // 2, core % 2
    sl = nsa_slopes()
    my = [4 * g + 2 * hh, 4 * g + 2 * hh + 1]
    h4 = my + [4 * g + 2 * (1 - hh), 4 * g + 2 * (1 - hh) + 1]
    NK = S // 128
    t = np.arange(S, dtype=np.float32)
    rrow = _bf(-sl[h4][:, None] * t[None, :])
    p = np.arange(128, dtype=np.float32)
    pos = 128.0 * np.arange(NK, dtype=np.float32)[None, :] + p[:, None]
    posb = np.stack([sl[h] * pos for h in my], axis=1).astype(np.float32)
    npos = 16.0 * (128.0 * np.arange(8, dtype=np.float32)[None, :] + p[:, None]) + 15.0
    cposb = np.stack([sl[h] * npos for h in h4], axis=1).astype(np.float32)
    cposb[0, :, 0] = NEG
    k = np.arange(128)[:, None]; q = np.arange(512)[None, :]
    causal = [np.where(128 * j + k <= q, 0.0, NEG) for j in range(4)]
    cmpm = [np.where(16 * k + 15 <= 512 * b + q, 0.0, NEG) for b in range(4)]
    winm = []
    for m in range(8):
        d = q + 512 - 128 * m - k
        winm.append(np.where((d >= 0) & (d < 512), 0.0, NEG))
    masks = _bf(np.concatenate(causal + cmpm + winm, axis=1))
    jr = np.arange(128)[:, None]; c = np.arange(64 * 128)[None, :]
    EB = _bf((jr == 2 * (c // 128) + (c % 128) // 64).astype(np.float32))
    Ml = np.zeros((128, 34), np.float32)
    for kk in range(128):
        if kk % 4 in (1, 2, 3):
            Ml[kk, kk // 4] = (1.0, 2.0, 1.0)[kk % 4 - 1]
    Ml[:, 32] = 1.0
    Ml = _bf(Ml)
    qq = np.arange(128)[:, None]; m_ = np.arange(510)[None, :]
    jp = m_ - 254; cur = qq // 64
    Frel = np.where(jp > cur, -1e30, np.where((jp == cur) | (jp == cur - 1), 1e6, 0.0)).astype(np.float32)
    identb = _bf(np.eye(128))
    return dict(rrow=rrow, posb=posb.reshape(128, -1), cposb=cposb.reshape(128, -1), masks=masks, EB=EB,
                Ml=Ml, Frel=Frel, identb=identb)


def build_nsa(S, core=None):
    myl = [0, 1]
    kb = KB(7)
    nc = kb.nc
    NT, NK = S // 512, S // 128
    NW = 646
    xT = kb.din("xT", [D_MODEL, S])
    w = kb.din("w", [D_MODEL, NW])
    w1 = kb.din("w1", [64, 2 * 32 * 64])
    w2 = kb.din("w2", [64, 128])
    pT = kb.din("pT", [64, 64])
    rrow_d = kb.din("rrow", [4, S], BF16)
    posb_d = kb.din("posb", [128, 2 * NK])
    cposb_d = kb.din("cposb", [128, 32])
    masks_d = kb.din("masks", [128, 16 * 512], BF16)
    EB_d = kb.din("EB", [128, 8192], BF16)
    Ml_d = kb.din("Ml", [128, 34], BF16)
    Frel_d = kb.din("Frel", [128, 510])
    identb_d = kb.din("identb", [128, 128], BF16)
    oc = kb.dout("oc", [2, 65, S]); osel = kb.dout("os", [2, 65, S]); ow = kb.dout("ow", [2, 65, S])
    gT = kb.dout("gT", [6, S])
    psb = nc.alloc_psum_tensor("psb", [128, 1024], BF16)

    def ld(name, d, shape, dt=F32):
        t_ = kb.sb(name, shape, dt)
        kb.dma(t_[:, :], d[:, :], "c_" + name, writes=[name])
        return t_
    posb = ld("posb", posb_d, [128, 2 * NK]); cposb = ld("cposb", cposb_d, [128, 32])
    masks = ld("masks", masks_d, [128, 16 * 512], BF16); EB = ld("EB", EB_d, [128, 8192], BF16)
    Ml = ld("Ml", Ml_d, [128, 34], BF16); Frel = ld("Frel", Frel_d, [128, 510])
    identb = ld("identb", identb_d, [128, 128], BF16)
    w2f = ld("w2f", w2, [64, 128]); pTf = ld("pTf", pT, [64, 64])
    w1b = kb.sb("w1b", [64, 4096], BF16); w2b = kb.sb("w2b", [64, 128], BF16); pTb = kb.sb("pTb", [64, 64], BF16)
    w1st = kb.sb("w1st", [64, 512])
    for i_ in range(8):
        kb.dma(w1st[:, :], w1[:, i_ * 512:(i_ + 1) * 512], "w1st", writes=["w1st"])
        kb.copy("dve", w1b[:, i_ * 512:(i_ + 1) * 512], w1st[:, :], reads=["w1st"], writes=["w1b"])
    kb.copy("dve", w2b[:, :], w2f[:, :], reads=["w2f"], writes=["w2b"])
    kb.copy("dve", pTb[:, :], pTf[:, :], reads=["pTf"], writes=["pTb"])
    cbias = kb.sb("cbias", [64, 2])
    for kv in range(2):
        b = kb.bank(0)
        for l in range(32):
            kb.mm(b[0:64, kv:kv + 1], w1b[:, kv * 2048 + l * 64: kv * 2048 + (l + 1) * 64],
                  pTb[:, kv * 32 + l: kv * 32 + l + 1], l == 0, l == 31, ["w1b", "pTb"], ["pb0"])
        kb.copy("dve", cbias[:, kv:kv + 1], b[0:64, kv:kv + 1], reads=["pb0"], writes=["cbias"])

    wbf = kb.sb("wbf", [128, NCH, NW], BF16)
    wst = [kb.sb(f"wst{i}", [128, NW]) for i in range(2)]
    for c in range(NCH):
        kb.dma(wst[c % 2][:, :], w[c * 128:(c + 1) * 128, :], f"ws{c % 2}", writes=[("wst", c % 2)])
        kb.copy(("dve", "pool")[c % 2], wbf[:, c, :], wst[c % 2][:, :], reads=[("wst", c % 2)], writes=["wbf"])

    Ska = kb.sb("Ska", [65, S], BF16); Sva = kb.sb("Sva", [128, NK, 65], BF16)
    Wka = kb.sb("Wka", [65, 1024], BF16); Wva = kb.sb("Wva", [128, 8, 65], BF16)
    Kca = kb.sb("Kca", [65, 1024], BF16); VcT = kb.sb("VcT", [64, 1024], BF16); Vca = kb.sb("Vca", [128, 8, 65], BF16)
    kb.memset("pool", Ska[64:65, :], 1.0, writes=["Ska1"]); kb.memset("pool", Wka[64:65, :], 1.0, writes=["Wka1"])
    kb.memset("pool", Sva[:, :, 64:65], 1.0, writes=["Sva1"]); kb.memset("pool", Wva[:, :, 64:65], 1.0, writes=["Wva1"])
    kb.memset("pool", Kca[:, :], 0.0, writes=["Kca"]); kb.memset("pool", Kca[64:65, :], 1.0, writes=["Kca"])
    kb.memset("pool", VcT[:, :], 0.0, writes=["VcT"]); kb.memset("pool", Vca[:, :, :], 0.0, writes=["Vca"])
    kb.memset("pool", Vca[:, :, 64:65], 1.0, writes=["Vca"])
    Qa = [kb.sb(f"Qa{h}", [65, 512], BF16) for h in range(4)]
    cbuf = [kb.sb(f"cbuf{i}", [64, 528], BF16) for i in range(2)]
    kb.memset("pool", cbuf[0][:, :], 0.0, writes=[("cbuf", 0)]); kb.memset("pool", cbuf[1][:, :], 0.0, writes=[("cbuf", 1)])
    gsb = kb.sb("gsb", [6, 512])
    sums = kb.sb("sums", [128, 4]); rs = kb.sb("rs", [128, 4])
    imp = kb.sb("imp", [128, 256]); score = kb.sb("score", [128, 256]); top8 = kb.sb("top8", [128, 8]); thr = kb.sb("thr", [128, 1])
    negsel = kb.sb("negsel", [128, 256], BF16); negselT = kb.sb("negselT", [128, 2, 512], BF16)
    NP = 3
    Pb = [kb.sb(f"P{i}", [128, 512], BF16) for i in range(NP)]
    tmp = [kb.sb(f"tmp{i}", [128, 512]) for i in range(2)]
    osb = [kb.sb(f"osb{i}", [65, 512]) for i in range(2)]
    cnt = {"i": 0, "o": 0}
    gx = {k_: kb.sb("gx_" + k_, [64, 32]) for k_ in ("xs", "x2", "u", "e")}
    hg = kb.sb("hg", [64, 32], BF16)

    def attn(T, items, Kc, kkey, Qh, qkey, Vc, vkey, out_dst, hook=None, ring=None):
        n = len(items)
        ob = 5 + cnt["o"] % 2
        oslot = cnt["o"] % 2
        cnt["o"] += 1
        po = kb.bank(ob)
        base = cnt["i"]
        cnt["i"] += n

        def s1(i):
            kt, bias, mask, extra = items[i]
            gi = base + i
            sbk = 2 + gi % 2
            ps = kb.bank(sbk)
            kc = kt if ring is None else kt % ring
            kb.mm(ps, Kc[0:65, kc * 128:(kc + 1) * 128], Qh[0:65, :], True, extra is None,
                  list(kkey(kt)) + [qkey], [f"pb{sbk}"])
            if extra is not None:
                kb.mm(ps, extra[0], extra[1], False, True, extra[2], [f"pb{sbk}"])
            pk = ("P", gi % NP)
            if mask is not None:
                tk = ("tmp", gi % 2)
                kb.tt("dve", tmp[gi % 2][:, :], ps, mask, ALU.add, reads=[f"pb{sbk}", "masks"], writes=[tk])
                kb.act(Pb[gi % NP][:, :], tmp[gi % 2][:, :], AF.Exp, [tk, "posb", "cposb"], [pk], bias=bias)
            else:
                kb.act(Pb[gi % NP][:, :], ps, AF.Exp, [f"pb{sbk}", "posb", "cposb"], [pk], bias=bias)

        def s2(i):
            kt = items[i][0]
            gi = base + i
            kc = kt if ring is None else kt % ring
            kb.mm(po[0:65, :], Vc[:, kc, 0:65], Pb[gi % NP][:, :], i == 0, i == n - 1,
                  list(vkey(kt)) + [("P", gi % NP)], [f"pb{ob}"])
            if hook is not None:
                hook(kt, Pb[gi % NP], ("P", gi % NP))
        for i in range(n + 2):
            if i < n:
                s1(i)
            if i >= 2:
                s2(i - 2)
        if out_dst is not None:
            kb.copy("dve", osb[oslot][:, :], po[0:65, :], reads=[f"pb{ob}"], writes=[("osb", oslot)])
            kb.dma(out_dst, osb[oslot][:, :], f"o{oslot}", reads=[("osb", oslot)])

    def consume(T, hb, hkeys):
        ts_ = slice(T * 512, (T + 1) * 512)
        for h in range(4):
            b = kb.bank(h % 2)
            for c in range(NCH):
                kb.mm(b[0:64, :], wbf[:, c, h * 64:(h + 1) * 64], hb[:, c, :], c == 0, c == NCH - 1,
                      ["wbf", hkeys[c]], [f"pb{h % 2}"])
            kb.ts("dve", Qa[h][0:64, :], b[0:64, :], 0.125, None, ALU.mult, reads=[f"pb{h % 2}"], writes=[("Qa", h)])
            kb.dma(Qa[h][64:65, :], rrow_d[h:h + 1, ts_], f"rr{h}", writes=[("Qa", h)])
        for kv in range(2):
            b = kb.bank(kv)
            kb.copy("pool", cbuf[kv][:, 0:16], cbuf[kv][:, 512:528], reads=[("cbuf", kv)], writes=[("cbuf", kv)])
            for c in range(NCH):
                kb.mm(b[0:64, :], wbf[:, c, 256 + kv * 64:256 + (kv + 1) * 64], hb[:, c, :], c == 0, c == NCH - 1,
                      ["wbf", hkeys[c]], [f"pb{kv}"])
            kb.copy("dve", cbuf[kv][:, 16:528], b[0:64, :], reads=[f"pb{kv}"], writes=[("cbuf", kv)])
        for i_, (dst, dkey) in enumerate(((Ska, "Ska"), (Wka, "Wka"))):
            b = kb.bank(i_)
            for c in range(NCH):
                kb.mm(b[0:64, :], wbf[:, c, 384 + i_ * 64:384 + (i_ + 1) * 64], hb[:, c, :], c == 0, c == NCH - 1,
                      ["wbf", hkeys[c]], [f"pb{i_}"])
            dsl = ts_ if i_ == 0 else slice((T % 2) * 512, (T % 2 + 1) * 512)
            kb.copy("dve", dst[0:64, dsl], b[0:64, :], reads=[f"pb{i_}"], writes=[(dkey, T)])
        b = kb.bank(0)
        for c in range(NCH):
            kb.mm(b[0:6, :], wbf[:, c, 640:646], hb[:, c, :], c == 0, c == NCH - 1, ["wbf", hkeys[c]], ["pb0"])
        kb.copy("dve", gsb[:, :], b[0:6, :], reads=["pb0"], writes=["gsb"])
        kb.dma(gT[:, ts_], gsb[:, :], "og", reads=["gsb"])
        for j in range(4):
            kt = 4 * T + j
            bi = j % 2
            b = kb.bank(bi)
            for c in range(NCH):
                kb.mm(b[:, 0:128], hb[:, c, j * 128:(j + 1) * 128], wbf[:, c, 512:640], c == 0, c == NCH - 1,
                      ["wbf", hkeys[c]], [f"pb{bi}"])
            kb.copy("dve", Sva[:, kt, 0:64], b[:, 0:64], reads=[f"pb{bi}"], writes=[("Sva", kt)])
            kb.copy("dve", Wva[:, kt % 8, 0:64], b[:, 64:128], reads=[f"pb{bi}"], writes=[("Wva", kt)])
        a_t = T // 4
        for kv in range(2):
            b = kb.bank(kv)
            cv = cbuf[kv][:, :].rearrange("p (n s) -> p n s", s=16)
            for l in range(32):
                rhs = cv[:, 0:32, l] if l < 16 else cv[:, 1:33, l - 16]
                kb.mm(b[0:64, 0:32], w1b[:, kv * 2048 + l * 64: kv * 2048 + (l + 1) * 64], rhs, l == 0, l == 31,
                      ["w1b", ("cbuf", kv)], [f"pb{kv}"])
            xs, x2, u, e = gx["xs"], gx["x2"], gx["u"], gx["e"]
            kb.ts("dve", xs[:, :], b[0:64, 0:32], cbias[:, kv:kv + 1], None, ALU.add, reads=[f"pb{kv}", "cbias"], writes=["gxs"])
            kb.tt("dve", x2[:, :], xs[:, :], xs[:, :], ALU.mult, reads=["gxs"], writes=["gx2"])
            kb.ts("dve", u[:, :], x2[:, :], 0.044715, 1.0, ALU.mult, ALU.add, reads=["gx2"], writes=["gu"])
            kb.tt("dve", u[:, :], u[:, :], xs[:, :], ALU.mult, reads=["gu", "gxs"], writes=["gu"])
            kb.act(e[:, :], u[:, :], AF.Exp, ["gu"], ["ge"], scale=-1.5957691216057308)
            kb.ts("dve", e[:, :], e[:, :], 1.0, None, ALU.add, reads=["ge"], writes=["ge"])
            kb.p.op("dve", lambda en, e=e: en.reciprocal(e[:, :], e[:, :]), ["ge"], ["ge"])
            kb.tt("dve", hg[:, :], xs[:, :], e[:, :], ALU.mult, reads=["gxs", "ge"], writes=["hg"])
            b2 = kb.bank(kv)
            kb.mm(b2[0:64, 64:96], w2b[:, kv * 64:(kv + 1) * 64], hg[:, :], True, True, ["w2b", "hg"], [f"pb{kv}"])
            if kv == 0:
                kb.copy("dve", Kca[0:64, 32 * T:32 * T + 32], b2[0:64, 64:96], reads=[f"pb{kv}"], writes=["Kca"])
            else:
                kb.copy("dve", VcT[0:64, 32 * T:32 * T + 32], b2[0:64, 64:96], reads=[f"pb{kv}"], writes=["VcT"])
                kb.tr(psb[:, 0:64], VcT[0:64, a_t * 128:(a_t + 1) * 128], identb[0:64, 0:64], ["VcT", "identb"], ["psb"])
                kb.copy("dve", Vca[:, a_t, 0:64], psb[:, 0:64], reads=["psb"], writes=["Vca"])
        bsel = T % 4
        for h in range(4):
            items = []
            for a in range(a_t + 1):
                mask = masks[:, (4 + bsel) * 512:(5 + bsel) * 512] if a == a_t else None
                items.append((a, cposb[:, h * 8 + a:h * 8 + a + 1], mask, None))

            def hook(a, P, pkey, h=h):
                bi = kb.bank(4)
                for qb in range(4):
                    kb.mm(bi[:, qb * 34:qb * 34 + 34], P[:, qb * 128:(qb + 1) * 128], Ml[:, 0:34], True, True,
                          [pkey, "Ml"], ["pb4"])
                for qb in range(4):
                    kb.copy("dve", IMPx[:, qb, a * 33:(a + 1) * 33], bi[:, qb * 34:qb * 34 + 33],
                            reads=["pb4"], writes=["IMPx"])
            dst = oc[myl.index(h), :, ts_] if h in myl else None
            attn(T, items, Kca, lambda a: ["Kca"], Qa[h], ("Qa", h), Vca, lambda a: ["Vca"], dst, hook)
            for qb in range(4):
                v = IMPx[:, qb, :].rearrange("p (a c) -> p a c", c=33)
                kb.reduce("dve", sums[:, qb:qb + 1], v[:, :, 32], ALU.add, reads=["IMPx"], writes=["sums"])
            kb.ts("dve", sums[:, :], sums[:, :], 1e-30, None, ALU.max, reads=["sums"], writes=["sums"])
            kb.p.op("dve", lambda en: en.reciprocal(rs[:, :], sums[:, :]), ["sums"], ["rs"])
            for qb in range(4):
                v = IMPx[:, qb, :].rearrange("p (a c) -> p a c", c=33)
                iv = impacc[:, qb, :].rearrange("p (a c) -> p a c", c=32)
                if h == 0:
                    kb.ts("dve", iv, v[:, :, 0:32], rs[:, qb:qb + 1], None, ALU.mult, reads=["IMPx", "rs"], writes=["impacc"])
                else:
                    kb.stt("dve", iv, v[:, :, 0:32], rs[:, qb:qb + 1], iv, ALU.mult, ALU.add,
                           reads=["IMPx", "rs", "impacc"], writes=["impacc"])
        for qb in range(4):
            B = 4 * T + qb
            kb.tt("dve", score[:, :], impacc[:, qb, :], Frel[:, 254 - 2 * B:510 - 2 * B], ALU.add, reads=["impacc", "Frel"], writes=["score"])
            kb.ts("dve", score[:, 0:1], score[:, 0:1], 1e6, None, ALU.add, reads=["score"], writes=["score"])
            kb.p.op("dve", lambda en: en.max(top8[:, :], score[:, :]), ["score"], ["top8"])
            kb.reduce("dve", thr[:, :], top8[:, :], ALU.min, reads=["top8"], writes=["thr"])
            kb.ts("dve", negsel[:, :], score[:, :], thr[:, 0:1], NEG, ALU.is_lt, ALU.mult, reads=["score", "thr"], writes=["negsel"])
            for jt in range(2):
                kb.tr(psb[:, 128 + jt * 128:256 + jt * 128], negsel[:, jt * 128:(jt + 1) * 128], identb[:, :],
                      ["negsel", "identb"], ["psb"])
                kb.copy("dve", negselT[:, jt, qb * 128:(qb + 1) * 128], psb[:, 128 + jt * 128:256 + jt * 128],
                        reads=["psb"], writes=["negselT"])
        for mi, h in enumerate(myl):
            items = []
            for kt in range(4 * T + 4):
                mask = masks[:, (kt - 4 * T) * 512:(kt - 4 * T + 1) * 512] if kt >= 4 * T else None
                extra = (EB[:, (kt % 64) * 128:(kt % 64 + 1) * 128], negselT[:, kt // 64, :], ["EB", "negselT"])
                items.append((kt, posb[:, mi * NK + kt:mi * NK + kt + 1], mask, extra))
            attn(T, items, Ska, lambda kt: [("Ska", kt // 4), "Ska1"], Qa[h], ("Qa", h), Sva,
                 lambda kt: [("Sva", kt), "Sva1"], osel[mi, :, ts_])
            items = []
            for m in range(8):
                kt = 4 * T - 4 + m
                if kt < 0:
                    continue
                items.append((kt, posb[:, mi * NK + kt:mi * NK + kt + 1], masks[:, (8 + m) * 512:(9 + m) * 512], None))
            attn(T, items, Wka, lambda kt: [("Wka", kt // 4), "Wka1"], Qa[h], ("Qa", h), Wva,
                 lambda kt: [("Wva", kt), "Wva1"], ow[mi, :, ts_], ring=8)

    IMPx = kb.sb("IMPx", [128, 4, 264])
    kb.memset("pool", IMPx[:, :, :], 0.0, writes=["IMPx"])
    impacc = kb.sb("impacc", [128, 4, 256])
    stream_project(kb, xT, S, 512, wbf, "wbf", consume, nbuf=1, NST=2)
    return kb.finish()


ALPHA = float((2 * 2) ** 0.25)
EPS = 1e-5


def tail_consts():
    identf = np.eye(128, dtype=np.float32)
    selE = np.zeros((16, 16 * 128), np.float32)
    for e in range(16):
        selE[e, e * 128:(e + 1) * 128] = 1.0
    return dict(identf=identf, identb=_bf(identf), selE=selE)


def build_tail(NTOK):
    kb = KB(7)
    nc = kb.nc
    psb = nc.alloc_psum_tensor("psb", [128, 1024], BF16)
    GB = min(2, NTOK // 128)
    GT = GB * 128
    NG = NTOK // GT
    hres_d = kb.din("hres", [NTOK, 2048])
    onsa_d = kb.din("onsa", [NTOK, 16 * 3 * 65])
    gat_d = kb.din("gat", [NTOK, 48])
    osb_d = kb.din("osb", [NTOK, 512])
    ofox_d = kb.din("ofox", [NTOK, 8 * 65])
    prm_d = kb.din("prm", [5, 128, 2048])
    wout_d = kb.din("wout", [2048, 2048])
    rw_d = kb.din("rw", [2048, 20]); rb_d = kb.din("rb", [128, 20])
    wg_d = kb.din("wg", [16, 2048, 256]); wu_d = kb.din("wu", [16, 2048, 256]); wd_d = kb.din("wd", [16, 256, 2048])
    identf_d = kb.din("identf", [128, 128]); identb_d = kb.din("identb", [128, 128], BF16)
    selE_d = kb.din("selE", [16, 2048])
    hout = kb.dout("hout", [NTOK, 2048])

    def ld(name, d, shape, dt=F32):
        t_ = kb.sb(name, shape, dt)
        kb.dma(t_[:, :], d[:, :], "c_" + name, writes=[name])
        return t_
    identf = ld("identf", identf_d, [128, 128]); identb = ld("identb", identb_d, [128, 128], BF16)
    selE = ld("selE", selE_d, [16, 2048]); rb = ld("rb", rb_d, [128, 20])
    rwf = kb.sb("rwf", [128, NCH, 20])
    kb.dma(rwf[:, :, :], rw_d.rearrange("(c p) n -> p c n", p=128), "c_rw", writes=["rwf"])

    prm = [kb.sb(f"prm{i}", [128, 2048]) for i in range(2)]
    pc = {"n": 0}

    def getprm(idx):
        s = pc["n"] % 2
        pc["n"] += 1
        kb.dma(prm[s][:, :], prm_d[idx, :, :], f"prm{s}", writes=[("prm", s)])
        return prm[s], ("prm", s)

    hres = kb.sb("hres", [128, GB, 2048]); ysb = kb.sb("ysb", [128, GB, 2048])
    onsa = kb.sb("onsa", [128, 16 * 3 * 65]); gat = kb.sb("gat", [128, 48]); ofox = kb.sb("ofox", [128, 8 * 65])
    y = kb.sb("y", [128, 2048]); ybf = kb.sb("ybf", [128, 2048], BF16)
    sq = kb.sb("sq", [128, 2048])
    yT = kb.sb("yT", [128, NCH, GT], BF16)
    h1Tf = kb.sb("h1Tf", [128, NCH, 128]); h1Tb = kb.sb("h1Tb", [128, NCH, GT], BF16)
    hid = kb.sb("hid", [128, 32, GT], BF16)
    wstg = [kb.sb(f"wstg{i}", [128, 2048]) for i in range(2)]
    wob = kb.sb("wob", [128, NCH, 512], BF16)
    wgb = kb.sb("wgb", [128, NCH, 256], BF16); wub = kb.sb("wub", [128, NCH, 256], BF16); wdb = kb.sb("wdb", [128, 2, 2048], BF16)
    sm = {k_: kb.sb("sm_" + k_, [128, 64]) for k_ in ("rsum", "eg", "coef", "ss", "lg", "t1", "t2", "gate", "top8")}
    gateT = kb.sb("gateT", [16, GT])
    kb.memset("dve", sm["t1"][:, :], 1.0, writes=["lnr0", "lnm", "lnv"])
    kb.memset("dve", sm["ss"][:, :], 1.0, writes=["ss"])
    t512 = [kb.sb(f"t512_{i}", [128, GT]) for i in range(2)]
    sc = {"w": 0}

    def stage_cast(dst_ap, src_ap, shape_cols, dkey):
        s = sc["w"] % 2
        sc["w"] += 1
        kb.dma(wstg[s][:, 0:shape_cols], src_ap, f"wstg{s}", writes=[("wstg", s)])
        kb.copy(("dve", "pool")[s], dst_ap, wstg[s][:, 0:shape_cols], reads=[("wstg", s)], writes=[dkey])

    def layer_norm(x_ap, xkey, gi, bi_, out_ap, okey):
        mean = sm["t1"][:, 0:1]; var = sm["t1"][:, 1:2]; rstd = sm["t1"][:, 6:7]
        kb.reduce("dve", mean, x_ap, ALU.add, reads=[xkey], writes=["lnm"])
        kb.ts("dve", mean, mean, 1.0 / 2048, None, ALU.mult, reads=["lnm"], writes=["lnm"])
        kb.ts("dve", x_ap, x_ap, mean, None, ALU.subtract, reads=[xkey, "lnm"], writes=[xkey])
        kb.tt("dve", sq[:, :], x_ap, x_ap, ALU.mult, reads=[xkey], writes=["sq"])
        kb.reduce("dve", var, sq[:, :], ALU.add, reads=["sq"], writes=["lnv"])
        kb.ts("dve", sm["t1"][:, 2:3], var, 1.0 / 2048, EPS, ALU.mult, ALU.add, reads=["lnv"], writes=["lnr0"])
        kb.act(sm["t1"][:, 4:6], sm["t1"][:, 2:4], AF.Ln, ["lnr0"], ["lnr1"])
        kb.act(sm["t1"][:, 6:8], sm["t1"][:, 4:6], AF.Exp, ["lnr1"], ["lnr"], scale=-0.5)
        gp, gk = getprm(gi)
        kb.stt("dve", x_ap, x_ap, rstd, gp[:, :], ALU.mult, ALU.mult, reads=[xkey, "lnr", gk], writes=[xkey])
        bp, bk = getprm(bi_)
        kb.tt("dve", out_ap, x_ap, bp[:, :], ALU.add, reads=[xkey, bk], writes=[okey])

    for G in range(NG):
        for bl in range(GB):
            r0 = G * GT + bl * 128
            rows = slice(r0, r0 + 128)
            kb.dma(hres[:, bl, :], hres_d[rows, :], f"hres{bl}", writes=[("hres", bl)])
            kb.dma(onsa[:, :], onsa_d[rows, :], "onsa", writes=["onsa"])
            kb.dma(gat[:, :], gat_d[rows, :], "gat", writes=["gat"])
            kb.dma(ofox[:, :], ofox_d[rows, :], "ofox", writes=["ofox"])
            kb.dma(y[:, 1024:1536], osb_d[rows, :], "osbl", writes=["y"])
            ov = onsa[:, :].rearrange("p (h d) -> p h d", d=65)
            rsum, eg, coef = sm["rsum"][:, 0:48], sm["eg"][:, 0:48], sm["coef"][:, 0:48]
            kb.ts("dve", rsum, ov[:, :, 64], 1e-30, None, ALU.max, reads=["onsa"], writes=["rsum"])
            kb.p.op("dve", lambda en, rsum=rsum: en.reciprocal(rsum, rsum), ["rsum"], ["rsum"])
            kb.act(eg, gat[:, :], AF.Exp, ["gat"], ["eg"], scale=-1.0)
            kb.ts("dve", eg, eg, 1.0, None, ALU.add, reads=["eg"], writes=["eg"])
            kb.p.op("dve", lambda en, eg=eg: en.reciprocal(eg, eg), ["eg"], ["eg"])
            kb.tt("dve", coef, eg, rsum, ALU.mult, reads=["eg", "rsum"], writes=["coef"])
            for h in range(16):
                yo = y[:, h * 64:(h + 1) * 64]
                for b in range(3):
                    hb_ = h * 3 + b
                    if b == 0:
                        kb.ts("dve", yo, ov[:, hb_, 0:64], coef[:, hb_:hb_ + 1], None, ALU.mult, reads=["onsa", "coef"], writes=["y"])
                    else:
                        kb.stt("dve", yo, ov[:, hb_, 0:64], coef[:, hb_:hb_ + 1], yo, ALU.mult, ALU.add,
                               reads=["onsa", "coef", "y"], writes=["y"])
            fv = ofox[:, :].rearrange("p (h d) -> p h d", d=65)
            rf = sm["rsum"][:, 48:56]
            kb.ts("dve", rf, fv[:, :, 64], 1e-30, None, ALU.max, reads=["ofox"], writes=["rf"])
            kb.p.op("dve", lambda en, rf=rf: en.reciprocal(rf, rf), ["rf"], ["rf"])
            for h in range(8):
                kb.ts("dve", y[:, 1536 + h * 64:1536 + (h + 1) * 64], fv[:, h, 0:64], rf[:, h:h + 1], None, ALU.mult,
                      reads=["ofox", "rf"], writes=["y"])
            nw, nk = getprm(0)
            grp = ((0, 1024), (1024, 1536), (1536, 2048))
            for gi_, (a0, a1) in enumerate(grp):
                ss = sm["ss"][:, gi_:gi_ + 1]
                kb.tt("dve", sq[:, a0:a1], y[:, a0:a1], y[:, a0:a1], ALU.mult, reads=["y"], writes=["sq"])
                kb.reduce("dve", ss, sq[:, a0:a1], ALU.add, reads=["sq"], writes=["ss"])
                kb.ts("dve", ss, ss, 1.0 / (a1 - a0), EPS, ALU.mult, ALU.add, reads=["ss"], writes=["ss"])
            kb.act(sm["ss"][:, 4:8], sm["ss"][:, 0:4], AF.Ln, ["ss"], ["ss1"])
            kb.act(sm["ss"][:, 8:12], sm["ss"][:, 4:8], AF.Exp, ["ss1"], ["ss2"], scale=-0.5)
            for gi_, (a0, a1) in enumerate(grp):
                kb.stt("dve", ybf[:, a0:a1], y[:, a0:a1], sm["ss"][:, 8 + gi_:9 + gi_], nw[:, a0:a1], ALU.mult, ALU.mult,
                       reads=["y", "ss2", nk], writes=["ybf"])
            for c0 in range(0, NCH, 4):
                for c in range(c0, c0 + 4):
                    kb.tr(psb[:, (c % 4) * 128:(c % 4 + 1) * 128], ybf[:, c * 128:(c + 1) * 128], identb[:, :], ["ybf", "identb"], ["psb"])
                for c in range(c0, c0 + 4):
                    kb.copy("dve", yT[:, c, bl * 128:(bl + 1) * 128], psb[:, (c % 4) * 128:(c % 4 + 1) * 128],
                            reads=["psb"], writes=["yT"])
        for n in range(4):
            for c4 in range(8):
                src = wout_d[c4 * 256:(c4 + 1) * 256, n * 512:(n + 1) * 512].rearrange("(c p) n -> p c n", p=128)
                s = sc["w"] % 2
                sc["w"] += 1
                kb.dma(wstg[s][:, 0:1024].rearrange("p (c n) -> p c n", n=512), src, f"wstg{s}", writes=[("wstg", s)])
                kb.copy(("dve", "pool")[s], wob[:, 2 * c4:2 * c4 + 2, :], wstg[s][:, 0:1024].rearrange("p (c n) -> p c n", n=512),
                        reads=[("wstg", s)], writes=["wob"])
            for bl in range(GB):
                bk_ = (bl + n * GB) % 2
                b = kb.bank(bk_)
                for c in range(NCH):
                    kb.mm(b, yT[:, c, bl * 128:(bl + 1) * 128], wob[:, c, :], c == 0, c == NCH - 1, ["yT", "wob"], [f"pb{bk_}"])
                kb.stt("dve", hres[:, bl, n * 512:(n + 1) * 512], hres[:, bl, n * 512:(n + 1) * 512], ALPHA, b, ALU.mult, ALU.add,
                       reads=[("hres", bl), f"pb{bk_}"], writes=[("hres", bl)])
        for bl in range(GB):
            layer_norm(hres[:, bl, :], ("hres", bl), 1, 2, hres[:, bl, :], ("hres", bl))
            for c in range(NCH):
                bk_ = 2 + c % 2
                b = kb.bank(bk_)
                kb.tr(b[:, 0:128], hres[:, bl, c * 128:(c + 1) * 128], identf[:, :], [("hres", bl), "identf"], [f"pb{bk_}"])
                kb.copy("dve", h1Tf[:, c, :], b[:, 0:128], reads=[f"pb{bk_}"], writes=["h1Tf"])
                kb.copy("pool", h1Tb[:, c, bl * 128:(bl + 1) * 128], h1Tf[:, c, :], reads=["h1Tf"], writes=["h1Tb"])
            b = kb.bank(4)
            for c in range(NCH):
                kb.mm(b[:, 0:20], h1Tf[:, c, :], rwf[:, c, :], c == 0, c == NCH - 1, ["h1Tf", "rwf"], ["pb4"])
            lg = sm["lg"][:, 0:20]
            kb.tt("dve", lg, b[:, 0:20], rb[:, :], ALU.add, reads=["pb4", "rb"], writes=["lg"])
            mx = sm["t2"][:, 0:1]; gw = sm["t2"][:, 1:2]; v1 = sm["t2"][:, 2:3]; w1_ = sm["t2"][:, 3:4]; w2_ = sm["t2"][:, 4:5]
            nmx = sm["t2"][:, 5:6]; dv = sm["t2"][:, 6:7]
            oh = sm["t2"][:, 8:12]; eg4 = sm["t2"][:, 12:16]; sel16 = sm["t2"][:, 16:32]; m1 = sm["t2"][:, 32:48]; m2 = sm["t2"][:, 48:64]
            kb.reduce("dve", mx, lg[:, 0:4], ALU.max, reads=["lg"], writes=["r_mx"])
            kb.ts("dve", oh, lg[:, 0:4], mx, None, ALU.is_ge, reads=["lg", "r_mx"], writes=["r_oh"])
            kb.ts("dve", nmx, mx, -1.0, None, ALU.mult, reads=["r_mx"], writes=["r_nmx"])
            kb.act(eg4, lg[:, 0:4], AF.Exp, ["lg", "r_nmx"], ["r_eg4"], bias=nmx)
            kb.reduce("dve", gw, eg4, ALU.add, reads=["r_eg4"], writes=["r_gw"])
            kb.p.op("dve", lambda en, gw=gw: en.reciprocal(gw, gw), ["r_gw"], ["r_gw"])
            kb.ts("dve", oh, oh, 1.0, 1e30, ALU.subtract, ALU.mult, reads=["r_oh"], writes=["r_oh"])
            for g_ in range(4):
                kb.ts("dve", sel16[:, g_ * 4:(g_ + 1) * 4], lg[:, 4 + g_ * 4:8 + g_ * 4], oh[:, g_:g_ + 1], None, ALU.add,
                      reads=["lg", "r_oh"], writes=["r_sel"])
            top8 = sm["top8"][:, 0:8]
            sel2 = sm["top8"][:, 16:32]
            kb.reduce("dve", top8[:, 0:1], sel16, ALU.max, reads=["r_sel"], writes=["r_top8"])
            kb.ts("dve", m1, sel16, top8[:, 0:1], None, ALU.is_equal, reads=["r_sel", "r_top8"], writes=["r_m1"])
            kb.stt("dve", sel2, m1, -1e30, sel16, ALU.mult, ALU.add, reads=["r_m1", "r_sel"], writes=["r_sel2"])
            kb.reduce("dve", top8[:, 1:2], sel2, ALU.max, reads=["r_sel2"], writes=["r_top8"])
            kb.ts("dve", m2, sel2, top8[:, 1:2], None, ALU.is_equal, reads=["r_sel2", "r_top8"], writes=["r_m2"])
            kb.tt("dve", dv, top8[:, 1:2], top8[:, 0:1], ALU.subtract, reads=["r_top8"], writes=["r_dv"])
            kb.act(w1_, dv, AF.Exp, ["r_dv"], ["r_w1"])
            kb.ts("dve", w1_, w1_, 1.0, None, ALU.add, reads=["r_w1"], writes=["r_w1"])
            kb.p.op("dve", lambda en, w1_=w1_: en.reciprocal(w1_, w1_), ["r_w1"], ["r_w1"])
            kb.ts("dve", w2_, w1_, -1.0, 1.0, ALU.mult, ALU.add, reads=["r_w1"], writes=["r_w2"])
            kb.tt("dve", w1_, w1_, gw, ALU.mult, reads=["r_w1", "r_gw"], writes=["r_w1"])
            kb.tt("dve", w2_, w2_, gw, ALU.mult, reads=["r_w2", "r_gw"], writes=["r_w2"])
            gate = sm["gate"][:, 0:16]
            kb.ts("dve", gate, m1, w1_, None, ALU.mult, reads=["r_m1", "r_w1"], writes=["gate"])
            kb.stt("dve", gate, m2, w2_, gate, ALU.mult, ALU.add, reads=["r_m2", "r_w2", "gate"], writes=["gate"])
            b = kb.bank(4)
            kb.tr(b[0:16, 128:256], gate, identf[:, :], ["gate", "identf"], ["pb4"])
            kb.copy("dve", gateT[:, bl * 128:(bl + 1) * 128], b[0:16, 128:256], reads=["pb4"], writes=["gateT"])
        for e in range(16):
            for (dst, src_d, key) in ((wgb, wg_d, "wgb"), (wub, wu_d, "wub")):
                for half in range(2):
                    s = sc["w"] % 2
                    sc["w"] += 1
                    kb.dma(wstg[s][:, :].rearrange("p (c n) -> p c n", n=256),
                           src_d[e, half * 1024:(half + 1) * 1024, :].rearrange("(c p) n -> p c n", p=128), f"wstg{s}", writes=[("wstg", s)])
                    kb.copy(("dve", "pool")[s], dst[:, half * 8:(half + 1) * 8, :], wstg[s][:, :].rearrange("p (c n) -> p c n", n=256),
                            reads=[("wstg", s)], writes=[key])
            for fc in range(2):
                s = sc["w"] % 2
                sc["w"] += 1
                kb.dma(wstg[s][:, :], wd_d[e, fc * 128:(fc + 1) * 128, :], f"wstg{s}", writes=[("wstg", s)])
                kb.copy(("dve", "pool")[s], wdb[:, fc, :], wstg[s][:, :], reads=[("wstg", s)], writes=["wdb"])
            for fc in range(2):
                bg, bu, bgt = kb.bank(2), kb.bank(3), kb.bank(4)
                for c in range(NCH):
                    kb.mm(bg[:, 0:GT], wgb[:, c, fc * 128:(fc + 1) * 128], h1Tb[:, c, :], c == 0, c == NCH - 1, ["wgb", "h1Tb"], ["pb2"])
                for c in range(NCH):
                    kb.mm(bu[:, 0:GT], wub[:, c, fc * 128:(fc + 1) * 128], h1Tb[:, c, :], c == 0, c == NCH - 1, ["wub", "h1Tb"], ["pb3"])
                kb.mm(bgt[:, 0:GT], selE[0:16, e * 128:(e + 1) * 128], gateT[0:16, :], True, True, ["selE", "gateT"], ["pb4"])
                ta, tb = t512[0], t512[1]
                kb.act(ta[:, :], bg[:, 0:GT], AF.Silu, ["pb2"], ["t512a"])
                kb.tt("dve", tb[:, :], ta[:, :], bu[:, 0:GT], ALU.mult, reads=["t512a", "pb3"], writes=["t512b"])
                kb.tt("dve", hid[:, e * 2 + fc, :], tb[:, :], bgt[:, 0:GT], ALU.mult, reads=["t512b", "pb4"], writes=[("hid", e)])
            for bl in range(GB):
                for n in range(4):
                    bk_ = 5 + (bl * 4 + n) % 2
                    b = kb.bank(bk_)
                    for fc in range(2):
                        kb.mm(b, hid[:, e * 2 + fc, bl * 128:(bl + 1) * 128], wdb[:, fc, n * 512:(n + 1) * 512], fc == 0, fc == 1,
                              [("hid", e), "wdb"], [f"pb{bk_}"])
                    ydst = ysb[:, bl, n * 512:(n + 1) * 512]
                    if e == 0:
                        kb.copy("dve", ydst, b, reads=[f"pb{bk_}"], writes=[("ysb", bl)])
                    else:
                        kb.tt("dve", ydst, ydst, b, ALU.add, reads=[f"pb{bk_}", ("ysb", bl)], writes=[("ysb", bl)])
        for bl in range(GB):
            r0 = G * GT + bl * 128
            kb.stt("dve", ysb[:, bl, :], hres[:, bl, :], ALPHA, ysb[:, bl, :], ALU.mult, ALU.add,
                   reads=[("hres", bl), ("ysb", bl)], writes=[("ysb", bl)])
            layer_norm(ysb[:, bl, :], ("ysb", bl), 3, 4, ysb[:, bl, :], ("ysb", bl))
            kb.dma(hout[r0:r0 + 128, :], ysb[:, bl, :], f"out{bl}", reads=[("ysb", bl)])
    return kb.finish()


_CACHE = {}
OFF = np.cumsum([0, 1024, 256, 256, 256, 256, 256, 256, 48, 512, 512, 512, 512, 512, 512, 8])


def _prog(name, fn):
    if name not in _CACHE:
        _CACHE[name] = fn()
    return _CACHE[name]


def _run(nc, in_maps):
    res = run_bass_kernel_spmd(nc, in_maps, core_ids=list(range(len(in_maps))))
    return res.results


def _layer(h, l, P, S):
    C = 8
    w_in = P["w_in"][l]
    col = lambda i: w_in[:, OFF[i]:OFF[i + 1]]
    hT = np.ascontiguousarray(h.T)
    fq, fk, fv, ff = col(11), col(12), col(13), col(14)
    cf, cb = fox_consts()
    ims = []
    for c in range(C):
        hs = slice(64 * c, 64 * c + 64)
        w = np.ascontiguousarray(np.concatenate([fq[:, hs], fk[:, hs], fv[:, hs], ff[:, c:c + 1]], axis=1))
        fb = np.ascontiguousarray(np.broadcast_to(P["fox_forget_bias"][l, c], (128, 1))).astype(np.float32)
        ims.append({"xT": hT, "w": w, "fb": fb, "cf": cf, "cb": cb})
    r_fox = _run(_prog(("fox", S), lambda: build_fox(S)), ims)
    sq_, sk_, sv_ = col(8), col(9), col(10)
    cbs = sb_consts()
    ims = []
    for c in range(C):
        hs = slice(64 * c, 64 * c + 64)
        w = np.ascontiguousarray(np.concatenate([sq_[:, hs], sk_[:, hs], sv_[:, hs]], axis=1))
        ims.append({"xT": hT, "w": w, "cb": cbs})
    r_sb = _run(_prog(("sb", S), lambda: build_sb(S)), ims)
    nq, ck, cv, sk, sv, wk, wv, ng = (col(i) for i in range(8))
    w1 = np.ascontiguousarray(np.concatenate([P["cmp_w1_k"][l].transpose(1, 0, 2).reshape(64, -1),
                                              P["cmp_w1_v"][l].transpose(1, 0, 2).reshape(64, -1)], axis=1))
    w2 = np.ascontiguousarray(np.concatenate([P["cmp_w2_k"][l], P["cmp_w2_v"][l]], axis=1))
    pT = np.ascontiguousarray(np.concatenate([P["cmp_pos_k"][l].T, P["cmp_pos_v"][l].T], axis=1))
    ims = []
    for c in range(C):
        g, hh = c // 2, c % 2
        my = [4 * g + 2 * hh, 4 * g + 2 * hh + 1]
        hord = my + [4 * g + 2 * (1 - hh), 4 * g + 2 * (1 - hh) + 1]
        gs = slice(64 * g, 64 * g + 64)
        w = np.concatenate([nq[:, 64 * h_:64 * h_ + 64] for h_ in hord] +
                           [ck[:, gs], cv[:, gs], sk[:, gs], wk[:, gs], sv[:, gs], wv[:, gs], ng[:, 3 * my[0]:3 * my[0] + 6]], axis=1)
        im = {"xT": hT, "w": np.ascontiguousarray(w), "w1": w1, "w2": w2, "pT": pT}
        im.update(nsa_consts(S, c))
        ims.append(im)
    r_nsa = _run(_prog(("nsa", S), lambda: build_nsa(S)), ims)
    onsa = np.empty((S, 16, 3, 65), np.float32)
    gat = np.empty((S, 16, 3), np.float32)
    for c in range(C):
        for mi in range(2):
            hd = 2 * c + mi
            for bi, nm in enumerate(("oc", "os", "ow")):
                onsa[:, hd, bi, :] = r_nsa[c][nm][mi].T
            gat[:, hd, :] = r_nsa[c]["gT"][3 * mi:3 * mi + 3].T
    osb = np.concatenate([r_sb[c]["oT"].T for c in range(C)], axis=1)
    ofox = np.concatenate([r_fox[c]["oT"].T for c in range(C)], axis=1)
    onsa = onsa.reshape(S, -1); gat = gat.reshape(S, -1)
    NTOK = S // C
    normw = np.concatenate([P["norm_nsa"][l], P["norm_sb"][l], P["norm_fox"][l]])
    prm = np.stack([np.broadcast_to(v, (128, 2048)) for v in
                    (normw, P["ln1_g"][l], P["ln1_b"][l], P["ln2_g"][l], P["ln2_b"][l])]).astype(np.float32)
    prm = np.ascontiguousarray(prm)
    rw = np.ascontiguousarray(np.concatenate([P["router_group_w"][l], P["router_expert_w"][l]], axis=1))
    rb = np.ascontiguousarray(np.broadcast_to(np.concatenate([P["router_group_b"][l], P["router_expert_b"][l]]), (128, 20))).astype(np.float32)
    tc_ = tail_consts()
    ims = []
    for c in range(C):
        rs_ = slice(c * NTOK, (c + 1) * NTOK)
        im = {"hres": np.ascontiguousarray(h[rs_]), "onsa": np.ascontiguousarray(onsa[rs_]), "gat": np.ascontiguousarray(gat[rs_]),
              "osb": np.ascontiguousarray(osb[rs_]), "ofox": np.ascontiguousarray(ofox[rs_]), "prm": prm,
              "wout": P["w_out"][l], "rw": rw, "rb": rb, "wg": P["expert_w_gate"][l], "wu": P["expert_w_up"][l],
              "wd": P["expert_w_down"][l]}
        im.update(tc_)
        ims.append(im)
    r_t = _run(_prog(("tail", NTOK), lambda: build_tail(NTOK)), ims)
    return np.concatenate([r_t[c]["hout"] for c in range(C)], axis=0)


def kernel(**inputs):
    P = {k: np.asarray(v) for k, v in inputs.items()}
    x = P["x"]
    S = x.shape[1]
    h = np.ascontiguousarray(x[0])
    for l in range(2):
        h = _layer(h, l, P, S)
    return h[None].astype(np.float32)
```
